# Optimizing a Trainium2 kernel written in Bass

```python
import jax, jax.numpy as jnp
from jax import lax
import numpy as np

D_MODEL = 2048
BATCH = 4
SEQ = 8192
DEPTH = 4

GRID_W = 64
N_BRANCH = 4
BRANCH_WIDTH = D_MODEL // 4
NA_HEADS = 8
NA_HEAD_DIM = BRANCH_WIDTH // NA_HEADS
NA_KH_MAX = 8
NA_KW = 16
NA_KBW = 2 * NA_KW
POOL_WINDOWS = (2, 4, 8, 16)
POOL_GROUPS = len(POOL_WINDOWS)
POOL_GROUP_DIM = BRANCH_WIDTH // POOL_GROUPS
SWA_Q_HEADS = 8
SWA_KV_HEADS = 2
SWA_HEAD_DIM = BRANCH_WIDTH // SWA_Q_HEADS
SWA_WINDOW = 128
SWA_BLOCK = 128
ROPE_THETA = 500000.0
ROPE_DIM = SWA_HEAD_DIM // 4
CONV_K = 3
MEM_LEN = 256
MEM_HEADS = 4
MEM_HEAD_DIM = D_MODEL // 16
D_FF = 4 * D_MODEL
LN_EPS = 1e-5
DEEPNORM_ALPHA = (2 * DEPTH) ** 0.25
DEEPNORM_BETA = (8 * DEPTH) ** -0.25
NA_COLS = 3 * BRANCH_WIDTH
POOL_COLS = BRANCH_WIDTH
SWA_Q_COLS = SWA_Q_HEADS * SWA_HEAD_DIM
SWA_KV_COLS = SWA_KV_HEADS * SWA_HEAD_DIM
SWA_COLS = SWA_Q_COLS + 2 * SWA_KV_COLS
CONV_COLS = 3 * BRANCH_WIDTH
IN_COLS = NA_COLS + POOL_COLS + SWA_COLS + CONV_COLS

kernel_name = 'hybrid_parallel_gated_encoder'


def layer_norm(x, g, b):
    xf = x.astype(jnp.float32)
    mu = xf.mean(-1, keepdims=True)
    var = jnp.square(xf - mu).mean(-1, keepdims=True)
    return ((xf - mu) * lax.rsqrt(var + LN_EPS) * g + b).astype(x.dtype)


def neighbourhood_attention(q, k, v, rpb):
    b, t, h, dh = q.shape
    rows = t // GRID_W
    kh = min(NA_KH_MAX, rows)
    ncb = GRID_W // NA_KW
    scale = dh ** -0.5
    qg = jnp.transpose(q.reshape(b, rows, ncb, NA_KW, h, dh), (1, 0, 4, 2, 3, 5))
    kg = jnp.transpose(k.reshape(b, rows, GRID_W, h, dh), (0, 3, 1, 2, 4))
    vg = jnp.transpose(v.reshape(b, rows, GRID_W, h, dh), (0, 3, 1, 2, 4))
    qcol = np.arange(GRID_W).reshape(ncb, NA_KW)
    win_start = np.clip(qcol - NA_KW // 2, 0, GRID_W - NA_KW)
    blk_start = np.clip(np.arange(ncb) * NA_KW - NA_KW // 2, 0, GRID_W - NA_KBW)
    kcol = blk_start[:, None] + np.arange(NA_KBW)[None, :]
    col_ok = (kcol[:, None, :] >= win_start[:, :, None]) & (kcol[:, None, :] < win_start[:, :, None] + NA_KW)
    dcol = np.clip(kcol[:, None, :] - qcol[:, :, None], -(NA_KW - 1), NA_KW - 1) + NA_KW - 1
    col_bias = rpb[:, :, dcol]
    mask = col_ok[:, :, None, :]

    def row_step(args):
        r, q_row = args
        rs = jnp.clip(r - kh // 2, 0, rows - kh)
        k_blk = lax.dynamic_slice_in_dim(kg, rs, kh, axis=2)[:, :, :, kcol, :]
        v_blk = lax.dynamic_slice_in_dim(vg, rs, kh, axis=2)[:, :, :, kcol, :]
        s = jnp.einsum('bhnqd,bhinkd->bhnqik', q_row, k_blk).astype(jnp.float32) * scale
        drow = rs + jnp.arange(kh) - r + NA_KH_MAX - 1
        bias = jnp.transpose(col_bias[:, drow], (0, 2, 3, 1, 4))
        s = jnp.where(mask, s + bias[None].astype(jnp.float32), -jnp.inf)
        p = jax.nn.softmax(s, axis=(-2, -1))
        return jnp.einsum('bhnqik,bhinkd->bhnqd', p.astype(v.dtype), v_blk)

    out = lax.map(row_step, (jnp.arange(rows), qg))
    return jnp.transpose(out, (1, 0, 3, 4, 2, 5)).reshape(b, t, h * dh)


def multiscale_pool(u, w_grp, scale):
    b, t, c = u.shape
    uf = u.astype(jnp.float32)
    csum = jnp.concatenate([jnp.zeros((b, 1, c), jnp.float32), jnp.cumsum(uf, axis=1)], axis=1)
    pos = jnp.arange(t)
    groups = []
    for g, w in enumerate(POOL_WINDOWS):
        lo = jnp.clip(pos - w // 2, 0, t)
        hi = jnp.clip(pos + w // 2, 0, t)
        cg = csum[:, :, g * POOL_GROUP_DIM:(g + 1) * POOL_GROUP_DIM]
        mean = (cg[:, hi] - cg[:, lo]) / (hi - lo).astype(jnp.float32)[None, :, None]
        groups.append(mean - uf[:, :, g * POOL_GROUP_DIM:(g + 1) * POOL_GROUP_DIM])
    y = jnp.stack(groups, axis=2).astype(u.dtype)
    y = jnp.einsum('btgc,gcd->btgd', y, w_grp).reshape(b, t, c)
    return y * scale


def rotary_partial(x):
    t = x.shape[1]
    half = ROPE_DIM // 2
    inv = ROPE_THETA ** (-jnp.arange(half, dtype=jnp.float32) / half)
    ang = jnp.arange(t, dtype=jnp.float32)[:, None] * inv[None, :]
    cos = jnp.cos(ang)[None, :, None, :]
    sin = jnp.sin(ang)[None, :, None, :]
    xr = x[..., :ROPE_DIM].astype(jnp.float32)
    x1, x2 = xr[..., :half], xr[..., half:]
    rot = jnp.concatenate([x1 * cos - x2 * sin, x2 * cos + x1 * sin], axis=-1)
    return jnp.concatenate([rot.astype(x.dtype), x[..., ROPE_DIM:]], axis=-1)


def windowed_gqa(q, k, v, sinks):
    b, t, hq, dh = q.shape
    hkv = k.shape[2]
    grp = hq // hkv
    nb = t // SWA_BLOCK
    qb = q.reshape(b, nb, SWA_BLOCK, hkv, grp, dh)
    pad = ((0, 0), (SWA_BLOCK, SWA_BLOCK), (0, 0), (0, 0))
    kp = jnp.pad(k, pad).reshape(b, nb + 2, SWA_BLOCK, hkv, dh)
    vp = jnp.pad(v, pad).reshape(b, nb + 2, SWA_BLOCK, hkv, dh)
    kw = jnp.concatenate([kp[:, :-2], kp[:, 1:-1], kp[:, 2:]], axis=2)
    vw = jnp.concatenate([vp[:, :-2], vp[:, 1:-1], vp[:, 2:]], axis=2)
    s = jnp.einsum('bnqhgd,bnkhd->bhgnqk', qb, kw).astype(jnp.float32) * (dh ** -0.5)
    qpos = jnp.arange(nb)[:, None] * SWA_BLOCK + jnp.arange(SWA_BLOCK)[None, :]
    kpos = jnp.arange(nb)[:, None] * SWA_BLOCK - SWA_BLOCK + jnp.arange(3 * SWA_BLOCK)[None, :]
    rel = kpos[:, None, :] - qpos[:, :, None]
    ok = (jnp.abs(rel) <= SWA_WINDOW) & (kpos[:, None, :] >= 0) & (kpos[:, None, :] < t)
    s = jnp.where(ok, s, -jnp.inf)
    sink = sinks.astype(jnp.float32).reshape(hkv, grp)[None, :, :, None, None, None]
    m = jnp.maximum(s.max(-1, keepdims=True), sink)
    p = jnp.exp(s - m)
    p = p / (p.sum(-1, keepdims=True) + jnp.exp(sink - m))
    o = jnp.einsum('bhgnqk,bnkhd->bnqhgd', p.astype(v.dtype), vw)
    return o.reshape(b, t, hq * dh)


def short_gated_conv(h, gate_b, gate_c, w_conv):
    t = h.shape[1]
    z = gate_c * h
    zp = jnp.pad(z, ((0, 0), (CONV_K // 2, CONV_K // 2), (0, 0)))
    y = zp[:, 0:t] * w_conv[0]
    for j in range(1, CONV_K):
        y = y + zp[:, j:j + t] * w_conv[j]
    return gate_b * y


def memory_cross_attention(x, mem, w_q, w_kv, w_o):
    b, t, _ = x.shape
    m = mem.shape[1]
    q = (x @ w_q).reshape(b, t, MEM_HEADS, MEM_HEAD_DIM)
    kv = (mem @ w_kv).reshape(b, m, 2, MEM_HEADS, MEM_HEAD_DIM)
    s = jnp.einsum('bthd,bmhd->bhtm', q, kv[:, :, 0]).astype(jnp.float32) * (MEM_HEAD_DIM ** -0.5)
    p = jax.nn.softmax(s, axis=-1).astype(x.dtype)
    o = jnp.einsum('bhtm,bmhd->bthd', p, kv[:, :, 1]).reshape(b, t, MEM_HEADS * MEM_HEAD_DIM)
    return o @ w_o


def setup_inputs(seed: int = 0) -> dict:
    key = jax.random.key(seed)
    ks = jax.random.split(key, 24)

    def nrm(k, shape, s):
        return jax.random.normal(k, shape, jnp.float32) * s

    L = DEPTH
    return {
        'x': nrm(ks[0], (BATCH, SEQ, D_MODEL), 1.0),
        'mem': nrm(ks[1], (BATCH, MEM_LEN, D_MODEL), 1.0),
        'w_in': nrm(ks[2], (L, D_MODEL, IN_COLS), D_MODEL ** -0.5),
        'na_rpb': nrm(ks[3], (L, NA_HEADS, 2 * NA_KH_MAX - 1, 2 * NA_KW - 1), 0.1),
        'pool_w': nrm(ks[4], (L, POOL_GROUPS, POOL_GROUP_DIM, POOL_GROUP_DIM), POOL_GROUP_DIM ** -0.5),
        'pool_scale': 1.0 + nrm(ks[5], (L, BRANCH_WIDTH), 0.1),
        'swa_sinks': nrm(ks[6], (L, SWA_Q_HEADS), 0.5),
        'conv_w': nrm(ks[7], (L, CONV_K, BRANCH_WIDTH), CONV_K ** -0.5),
        'w_branch': nrm(ks[8], (L, N_BRANCH, BRANCH_WIDTH, D_MODEL), BRANCH_WIDTH ** -0.5),
        'w_gate': nrm(ks[9], (L, N_BRANCH, D_MODEL, D_MODEL), D_MODEL ** -0.5),
        'b_gate': nrm(ks[10], (L, N_BRANCH, D_MODEL), 0.02),
        'w_mix_out': nrm(ks[11], (L, D_MODEL, D_MODEL), D_MODEL ** -0.5 * DEEPNORM_BETA),
        'ln1_g': 1.0 + nrm(ks[12], (L, D_MODEL), 0.02),
        'ln1_b': nrm(ks[13], (L, D_MODEL), 0.02),
        'wq_mem': nrm(ks[14], (L, D_MODEL, MEM_HEADS * MEM_HEAD_DIM), D_MODEL ** -0.5),
        'wkv_mem': nrm(ks[15], (L, D_MODEL, 2 * MEM_HEADS * MEM_HEAD_DIM), D_MODEL ** -0.5),
        'wo_mem': nrm(ks[16], (L, MEM_HEADS * MEM_HEAD_DIM, D_MODEL), (MEM_HEADS * MEM_HEAD_DIM) ** -0.5 * DEEPNORM_BETA),
        'ln2_g': 1.0 + nrm(ks[17], (L, D_MODEL), 0.02),
        'ln2_b': nrm(ks[18], (L, D_MODEL), 0.02),
        'w_ff1': nrm(ks[19], (L, D_MODEL, D_FF), D_MODEL ** -0.5),
        'w_ff2': nrm(ks[20], (L, D_FF, D_MODEL), D_FF ** -0.5 * DEEPNORM_BETA),
        'ln3_g': 1.0 + nrm(ks[21], (L, D_MODEL), 0.02),
        'ln3_b': nrm(ks[22], (L, D_MODEL), 0.02),
    }


def reference(x, mem, w_in, na_rpb, pool_w, pool_scale, swa_sinks, conv_w, w_branch, w_gate, b_gate,
              w_mix_out, ln1_g, ln1_b, wq_mem, wkv_mem, wo_mem, ln2_g, ln2_b, w_ff1, w_ff2, ln3_g, ln3_b):
    b, t, _ = x.shape
    o_pool = NA_COLS
    o_swa = o_pool + POOL_COLS
    o_conv = o_swa + SWA_COLS
    for l in range(DEPTH):
        u = x
        proj = u @ w_in[l]
        qkv_a = proj[..., :NA_COLS].reshape(b, t, 3, NA_HEADS, NA_HEAD_DIM)
        y_a = neighbourhood_attention(qkv_a[:, :, 0], qkv_a[:, :, 1], qkv_a[:, :, 2], na_rpb[l])
        y_b = multiscale_pool(proj[..., o_pool:o_swa], pool_w[l], pool_scale[l])
        sw = proj[..., o_swa:o_conv]
        q_c = rotary_partial(sw[..., :SWA_Q_COLS].reshape(b, t, SWA_Q_HEADS, SWA_HEAD_DIM))
        k_c = rotary_partial(sw[..., SWA_Q_COLS:SWA_Q_COLS + SWA_KV_COLS].reshape(b, t, SWA_KV_HEADS, SWA_HEAD_DIM))
        v_c = sw[..., SWA_Q_COLS + SWA_KV_COLS:].reshape(b, t, SWA_KV_HEADS, SWA_HEAD_DIM)
        y_c = windowed_gqa(q_c, k_c, v_c, swa_sinks[l])
        cv = proj[..., o_conv:]
        y_d = short_gated_conv(cv[..., :BRANCH_WIDTH], cv[..., BRANCH_WIDTH:2 * BRANCH_WIDTH],
                               cv[..., 2 * BRANCH_WIDTH:], conv_w[l])
        mix = jnp.zeros_like(u)
        for i, y in enumerate((y_a, y_b, y_c, y_d)):
            gate = jax.nn.sigmoid(u @ w_gate[l, i] + b_gate[l, i])
            mix = mix + gate * (y @ w_branch[l, i])
        x = layer_norm(DEEPNORM_ALPHA * x + mix @ w_mix_out[l], ln1_g[l], ln1_b[l])
        x = layer_norm(DEEPNORM_ALPHA * x + memory_cross_attention(x, mem, wq_mem[l], wkv_mem[l], wo_mem[l]),
                       ln2_g[l], ln2_b[l])
        ff = jnp.square(jax.nn.relu(x @ w_ff1[l])) @ w_ff2[l]
        x = layer_norm(DEEPNORM_ALPHA * x + ff, ln3_g[l], ln3_b[l])
    return x
```

```python
import numpy as np
from contextlib import ExitStack
import concourse.bass as bass
import concourse.mybir as mybir
from concourse.bass_utils import run_bass_kernel_spmd

F32 = mybir.dt.float32
BF16 = mybir.dt.bfloat16
AF = mybir.ActivationFunctionType
ALU = mybir.AluOpType

D = 2048
NCH = 16
T = 8192
TLOC = 5120
DEPTH = 4
VL = [4864, 4608, 4352, 4096]
ALPHA = float(8 ** 0.25)
EPS = 1e-5
NEG = -30000.0
CW = 176
CG = 136
WCOLS = 2560
NSLOT = 5
NDS = 16

KINDS = {
    "wgb": (64, 2560), "wout": (16, 2048), "wq": (4, 2048), "wkvk": (4, 2048), "wkvv": (4, 2048),
    "wo": (4, 2048), "w1": (64, 2048), "w2": (64, 2048), "win": (34, 2048),
}


class Buf:
    __slots__ = ("name", "w", "r")

    def __init__(self, name=""):
        self.name = name
        self.w = None
        self.r = {}


class Sched:
    LIMIT = 30000

    def __init__(self, nc):
        self.nc = nc
        self.E = {"pe": nc.tensor, "act": nc.scalar, "dve": nc.vector, "pool": nc.gpsimd, "sp": nc.sync}
        self.sems = []
        self.semidx = {}
        self.cnt = {}
        self.pe_sems = set()
        for e in ("pe", "act", "dve", "pool"):
            self._new_epoch(e)
        self.dsi = []
        for i in range(NDS):
            self.sems.append(nc.alloc_semaphore(name=f"dq{i}"))
            self.dsi.append(len(self.sems) - 1)
        self.dgen = [0] * NDS
        self.dnext = 0
        self.waited = {e: {} for e in self.E}
        self.arena = None
        self.nops = 0
        self.pool_to_dve = False

    def _new_epoch(self, e):
        self.sems.append(self.nc.alloc_semaphore(name=f"s_{e}_{len(self.sems)}"))
        self.semidx[e] = len(self.sems) - 1
        self.cnt[e] = 0
        if e == "pe":
            self.pe_sems.add(self.semidx[e])

    def new_sem(self, name):
        self.sems.append(self.nc.alloc_semaphore(name=name))
        return len(self.sems) - 1

    def op(self, eng, fn, reads=(), writes=(), dma=False, arena=True, sem=None):
        if eng == "pool" and self.pool_to_dve:
            eng = "dve"
        deps = {}

        def add(tok):
            if tok is None:
                return
            s, v = tok
            if deps.get(s, 0) < v:
                deps[s] = v

        rs = list(reads)
        if arena and self.arena is not None:
            rs.append(self.arena)
        for b in rs:
            add(b.w)
        for b in writes:
            add(b.w)
            for s, v in b.r.items():
                add((s, v))
        j = None
        if dma and sem is None:
            j = self.dnext
            self.dnext = (j + 1) % NDS
            if self.dgen[j] > 0:
                add((self.dsi[j], 16 * self.dgen[j]))
        wt = self.waited[eng]
        E = self.E[eng]
        for s, v in deps.items():
            if eng == "pe" and s in self.pe_sems:
                continue
            if wt.get(s, 0) >= v:
                continue
            E.wait_ge(self.sems[s], v)
            wt[s] = v
        ins = fn(E)
        if dma:
            if sem is not None:
                si, val = sem
                tok = (si, val)
            else:
                self.dgen[j] += 1
                tok = (self.dsi[j], 16 * self.dgen[j])
            ins.then_inc(self.sems[tok[0]], 16)
        else:
            self.cnt[eng] += 1
            tok = (self.semidx[eng], self.cnt[eng])
            ins.then_inc(self.sems[tok[0]], 1)
            if self.cnt[eng] >= self.LIMIT:
                self._new_epoch(eng)
        for b in rs:
            if b.r.get(tok[0], 0) < tok[1]:
                b.r[tok[0]] = tok[1]
        for b in writes:
            b.w = tok
            b.r = {}
        self.nops += 1
        return tok

    def finish(self):
        E = self.E["sp"]
        for j in range(NDS):
            if self.dgen[j] > 0:
                E.wait_ge(self.sems[self.dsi[j]], 16 * self.dgen[j])
        for e in ("pe", "act", "dve", "pool"):
            if self.cnt[e] > 0:
                E.wait_ge(self.sems[self.semidx[e]], self.cnt[e])


def bcast(ap, axis, n):
    dims = [list(d) for d in ap.ap]
    dims.insert(axis, [0, n])
    return bass.AP(ap.tensor, ap.offset, dims)


def groups_of(v):
    gs = []
    a = 0
    while a < v:
        w = min(512, v - a)
        gs.append((a, w))
        a += w
    return gs


def build_program(L=DEPTH, debug=False, cfg=None):
    cfg = cfg or {}
    nc = bass.Bass("TRN2", target_bir_lowering=False)
    S = Sched(nc)
    st = ExitStack()

    def din(name, shape, dt=F32):
        return nc.dram_tensor(name, list(shape), dt, kind="ExternalInput").ap()

    def dscr(name, shape, dt, out=False):
        return nc.dram_tensor(name, list(shape), dt, kind=("ExternalOutput" if out else "Internal")).ap()

    def sb(name, shape, dt):
        return st.enter_context(nc.sbuf_tensor(name, list(shape), dt))

    xin = din("xin", [NCH, 128, TLOC])
    memT = din("memT", [NCH, 128, 256])
    wf = {k: din(k + "_f", [L, n, 128, c]) for k, (n, c) in KINDS.items()}
    wb = {k: dscr(k + "_b", [L, n, 128, c], BF16) for k, (n, c) in KINDS.items()}
    nab_f = din("nab_f", [L, 128, 13 * 8 * 128])
    nab_b = dscr("nab_b", [L, 128, 13 * 8 * 128], BF16)
    poolw_f = din("poolw_f", [128, L * 512])
    cf_in = din("cf", [128, L * CW + CG])
    cb_in = din("cb_f", [128, 768])
    sink_in = din("sinkrow", [1, L * 8])
    cos_in = din("cosT", [128, TLOC])
    sin_in = din("sinT", [128, TLOC])
    PT = [dscr(f"PT{p}", [29, 128, TLOC], BF16, out=debug) for p in range(2)]
    VNA = [dscr(f"VNA{p}", [TLOC // 128, 128, 512], BF16, out=debug) for p in range(2)]
    VSW = [dscr(f"VSW{p}", [TLOC // 128, 128, 128], BF16, out=debug) for p in range(2)]
    XR = dscr("XR", [NCH, 128, VL[0]], F32, out=debug)
    YD = dscr("YD", [NCH, 128, VL[0]], BF16, out=True) if debug else None
    outT = dscr("outT", [NCH, 128, 4096], F32, out=True)
    PROJ = [[Buf(f"proj{p}_{g}") for g in range(10)] for p in range(2)]
    XRb = [Buf(f"xr{g}") for g in range(10)]
    OUTb = Buf("out")

    xT = sb("xT", [128, NCH, 512], F32)
    uT = sb("uT", [128, NCH, 512], BF16)
    yT = sb("yT", [128, NCH, 512], BF16)
    wsl = sb("wsl", [128, NSLOT, WCOLS], BF16)
    cf = sb("cfs", [128, L * CW + CG], F32)
    cb = sb("cbs", [128, 768], BF16)
    poolw = sb("poolw", [128, L * 512], BF16)
    KT = sb("KT", [128, 4, 256], BF16)
    Vm = sb("Vm", [128, 2, 512], BF16)
    sinkf = sb("sinkf", [1, L * 8], F32)
    sinke = sb("sinke", [1, L * 8], BF16)
    ps = st.enter_context(nc.psum_tensor("ps", [128, 8, 512], F32))
    xTb = [Buf(f"xT{i}") for i in range(NCH)]
    uTb = [Buf(f"uT{i}") for i in range(NCH)]
    yTb = [Buf(f"yT{i}") for i in range(NCH)]
    wslb = [Buf(f"ws{i}") for i in range(NSLOT)]
    psb = [Buf(f"ps{i}") for i in range(8)]
    cfb, cbb, poolwb, membb, KTb, Vmb, sinkb = (Buf("cf"), Buf("cb"), Buf("poolw"), Buf("memb"), Buf("KT"),
                                                Buf("Vm"), Buf("sink"))
    ARENA = Buf("arena")

    ident = cb[:, 0:128]
    Rt = cb[:, 128:256]
    maskq = [cb[:, 256:384], cb[:, 384:512]]
    ones = cb[:, 512:640]
    onesdiv = cb[:, 640:768]

    S.arena = ARENA
    S.op("sp", lambda e: e.dma_start(out=cf[:], in_=cf_in), writes=[cfb], dma=True)
    S.op("sp", lambda e: e.dma_start(out=sinkf[:], in_=sink_in), writes=[sinkb], dma=True)
    S.op("act", lambda e: e.activation(out=sinke[:], in_=sinkf[:], func=AF.Exp), reads=[sinkb], writes=[sinkb])
    memb_b = dscr("memb_b", [128, NCH, 256], BF16)
    membdb = Buf("membd")
    WB = {}
    order = []
    for l in range(L):
        ks = ["win"] if l == 0 else []
        ks += ["wkvk", "wkvv", "wgb", "wout", "wq", "wo", "w1", "w2"]
        order += [(k, l) for k in ks]
        order.append(("nab", l))
        if l + 1 < L:
            order.append(("win", l + 1))
    jobs = []
    jobs.append((cb_in, 768, cb[:], None, "cb"))
    jobs.append((poolw_f[:, 0:L * 512 // 2], L * 512 // 2, poolw[:, 0:L * 512 // 2], None, "poolw"))
    jobs.append((poolw_f[:, L * 512 // 2:], L * 512 // 2, poolw[:, L * 512 // 2:], None, "poolw"))
    for hf in range(4):
        jobs.append((memT[hf * 4:hf * 4 + 4].rearrange("c p n -> p c n"), 1024, None, memb_b[:, hf * 4:hf * 4 + 4, :], "memb"))
    marks = {}
    for (k, l) in order:
        if k == "nab":
            for c0 in range(0, 13 * 1024, 1280):
                c1 = min(13 * 1024, c0 + 1280)
                jobs.append((nab_f[l][:, c0:c1], c1 - c0, None, nab_b[l][:, c0:c1], (k, l)))
        else:
            n, c = KINDS[k]
            h = c // 2
            for idx in range(n):
                for hh in range(2):
                    jobs.append((wf[k][l, idx][:, hh * h:(hh + 1) * h], h, None, wb[k][l, idx][:, hh * h:(hh + 1) * h], (k, l)))
        marks[(k, l)] = len(jobs)
    for (k, l) in order:
        WB[(k, l)] = Buf(f"{k}{l}")
    keysem = {}
    keycnt = {}
    keytot = {}
    for j in jobs:
        if j[3] is not None:
            keytot[j[4]] = keytot.get(j[4], 0) + 1
    for key in keytot:
        keysem[key] = S.new_sem(f"c_{key}")
        keycnt[key] = 0
        tok = (keysem[key], 16 * keytot[key])
        if key == "memb":
            membdb.w = tok
        else:
            WB[key].w = tok
    p0s = {"ld": 0, "cs": 0, "st": 0, "stf": None}

    def p0_bind(stf, stbf, nbuf=2):
        p0s["stf"], p0s["stb"] = stf, stbf
        p0s["nb"] = nbuf
        p0s["fb"] = [Buf() for _ in range(nbuf)]
        p0s["bb"] = [Buf() for _ in range(nbuf)]
        p0s["base"] = p0s["cs"]
        assert p0s["ld"] == p0s["cs"] == p0s["st"]

    def p0_load(n):
        src, c, dst, store, key = jobs[n]
        r = n % p0s["nb"]
        stf = p0s["stf"]
        if len(src.shape) == 3:
            o = stf[:, r, 0:c].rearrange("p (a b) -> p a b", a=src.shape[1])
        else:
            o = stf[:, r, 0:c]
        S.op("sp", lambda e: e.dma_start(out=o, in_=src), writes=[p0s["fb"][r]], dma=True)

    def p0_cast(n):
        src, c, dst, store, key = jobs[n]
        r = n % p0s["nb"]
        stf, stbf = p0s["stf"], p0s["stb"]
        eng = "act" if n % 2 == 0 else "dve"
        if dst is not None:
            o, wr = dst, [{"cb": cbb, "poolw": poolwb}[key]]
        else:
            o, wr = stbf[:, r, 0:c], [p0s["bb"][r]]
        if eng == "act":
            S.op(eng, lambda e: e.activation(out=o, in_=stf[:, r, 0:c], func=AF.Copy), reads=[p0s["fb"][r]], writes=wr)
        else:
            S.op(eng, lambda e: e.tensor_copy(out=o, in_=stf[:, r, 0:c]), reads=[p0s["fb"][r]], writes=wr)

    def p0_store(n):
        src, c, dst, store, key = jobs[n]
        if store is None:
            return
        r = n % p0s["nb"]
        stbf = p0s["stb"]
        keycnt[key] += 1
        if len(store.shape) == 3:
            i_ = stbf[:, r, 0:c].rearrange("p (a b) -> p a b", a=store.shape[1])
        else:
            i_ = stbf[:, r, 0:c]
        S.op("sp", lambda e: e.dma_start(out=store, in_=i_), reads=[p0s["bb"][r]], dma=True, sem=(keysem[key], 16 * keycnt[key]))

    def p0_advance(target, flush=False):
        target = min(target, len(jobs))
        while p0s["cs"] < target:
            n = p0s["cs"]
            while p0s["ld"] < min(target, n + p0s["nb"]) or p0s["ld"] <= n:
                p0_load(p0s["ld"])
                p0s["ld"] += 1
            p0_cast(n)
            p0s["cs"] = n + 1
            if p0s["st"] < n:
                p0_store(p0s["st"])
                p0s["st"] += 1
        if flush:
            while p0s["st"] < p0s["cs"]:
                p0_store(p0s["st"])
                p0s["st"] += 1

    n_up = marks[("win", 0)] if cfg.get("p0", True) else 7
    if not cfg.get("p0", True):
        jobs = jobs[:7]
    with ExitStack() as p0:
        stf0 = p0.enter_context(nc.sbuf_tensor("p0f", [128, 6, 1280], F32))
        stb0 = p0.enter_context(nc.sbuf_tensor("p0b", [128, 6, 1280], BF16))
        p0_bind(stf0, stb0, 6)
        p0_advance(n_up, flush=True)
        S.op("dve", lambda e: e.memset(stf0[0:1, 0, 0:1], 0.0), writes=[ARENA, p0s["fb"][0]], arena=False)
    def mark_after_layer(l):
        if ("win", l + 1) in marks:
            return marks[("win", l + 1)]
        return len(jobs)
    sched_p0 = {"lo": n_up, "hi": n_up, "steps": 1, "i": 0}

    def p0_phase(hi, steps):
        sched_p0["lo"] = p0s["cs"]
        sched_p0["hi"] = min(hi, len(jobs))
        sched_p0["steps"] = max(1, steps)
        sched_p0["i"] = 0

    def p0_tick(flush=False):
        sched_p0["i"] += 1
        f = min(1.0, sched_p0["i"] / sched_p0["steps"])
        tgt = sched_p0["lo"] + int(round(f * (sched_p0["hi"] - sched_p0["lo"])))
        if tgt > p0s["cs"] or flush:
            p0_advance(max(tgt, p0s["cs"]), flush=flush)

    S.pool_to_dve = True
    seq = []

    def seq_group(l, tok, proj):
        s = []
        if tok:
            s += [("wgb", l, i) for i in range(64)]
            s += [("wout", l, i) for i in range(16)]
            s += [("wq", l, i) for i in range(4)]
            s += [("wo", l, i) for i in range(4)]
            for qq in range(4):
                s += [("w1", l, qq * 16 + i) for i in range(16)]
                s += [("w2", l, qq * 16 + i) for i in range(16)]
        if proj:
            s += [("win", l + 1 if tok else l, i) for i in range(34)]
        return s

    for _ in groups_of(TLOC):
        seq += seq_group(0, False, True)
    for l in range(L):
        seq += [("wkvk", l, i) for i in range(4)] + [("wkvv", l, i) for i in range(4)]
        for _ in groups_of(VL[l]):
            seq += seq_group(l, True, l + 1 < L)
    wstate = {"issued": 0, "taken": 0}

    def w_issue_upto(n):
        while wstate["issued"] < min(n, len(seq)):
            i = wstate["issued"]
            k, l, idx = seq[i]
            slot = i % NSLOT
            c = KINDS[k][1]
            S.op("sp", lambda e, k=k, l=l, idx=idx, slot=slot, c=c: e.dma_start(out=wsl[:, slot, 0:c], in_=wb[k][l, idx]),
                 reads=[WB[(k, l)]], writes=[wslb[slot]], dma=True, arena=False)
            wstate["issued"] += 1

    def w_take(kind, l, idx):
        i = wstate["taken"]
        assert seq[i] == (kind, l, idx), (seq[i], kind, l, idx)
        w_issue_upto(i + NSLOT)
        wstate["taken"] += 1
        slot = i % NSLOT
        return wsl[:, slot, :], wslb[slot]

    def mm(out, lhsT, rhs, start, stop, reads, writes):
        S.op("pe", lambda e: e.matmul(out, lhsT, rhs, start=start, stop=stop), reads=reads, writes=writes)

    rr = {"ev": 0}

    def lcol(l, off):
        return l * CW + off

    def mixers(l, g, a, Gw):
        par = l % 2
        npr = Gw // 128
        i0 = a // 128
        deps = [PROJ[par][gg] for gg in (g - 1, g, g + 1) if 0 <= gg < 10]
        with ExitStack() as ms:
            def msb(name, shape, dt):
                return ms.enter_context(nc.sbuf_tensor(f"{name}_{l}_{g}", list(shape), dt))
            naQ = msb("naQ", [128, 4, Gw], BF16)
            naK = msb("naK", [128, 4, Gw + 512], BF16)
            naV = msb("naV", [128, npr + 4, 512], BF16)
            nab = msb("nab", [128, 13, 8, 128], BF16)
            swQ = msb("swQ", [128, 4, Gw], BF16)
            swK = msb("swK", [128, Gw + 256], BF16)
            swV = msb("swV", [128, npr + 2, 128], BF16)
            pin = msb("pin", [128, 4, Gw + 24], BF16)
            cin = msb("cin", [128, 12, Gw + 2], BF16)
            Pna = msb("Pna", [128, 2, 1280], BF16)
            Psw = msb("Psw", [128, 2, 3, 512], BF16)
            rec = msb("rec", [128, 2, 512], F32)
            pw = msb("pw", [128, 3, Gw + 24], F32)
            pooled = msb("pooled", [128, 2, Gw], BF16)
            cw_ = msb("cw", [128, 1, 2, Gw + 2], F32)
            bq, bk, bv, bnab, bsq, bsk, bsv, bpin, bcin = [Buf() for _ in range(9)]
            bP = [Buf(), Buf()]
            bPs = [Buf(), Buf()]
            brec = [Buf(), Buf()]
            bpw = [Buf() for _ in range(3)]
            bpooled = [Buf(), Buf()]
            bcw = [[Buf(), Buf()]]
            b_hi = a + Gw
            klo = max(0, a - 256)
            S.op("sp", lambda e: e.dma_start(out=naQ[:], in_=PT[par][0:4, :, a:b_hi].rearrange("c p n -> p c n")),
                 reads=deps, writes=[bq], dma=True)
            S.op("sp", lambda e: e.dma_start(out=naK[:, :, klo - (a - 256):Gw + 512],
                                             in_=PT[par][4:8, :, klo:b_hi + 256].rearrange("c p n -> p c n")),
                 reads=deps, writes=[bk], dma=True)
            c_lo = max(0, i0 - 2)
            S.op("sp", lambda e: e.dma_start(out=naV[:, c_lo - (i0 - 2):npr + 4, :],
                                             in_=VNA[par][c_lo:i0 + npr + 2].rearrange("c p n -> p c n")),
                 reads=deps, writes=[bv], dma=True)
            ntile = 13 if g == 0 else 5
            S.op("sp", lambda e: e.dma_start(out=nab[:, 0:ntile].rearrange("p t h k -> p (t h k)"),
                                             in_=nab_b[l][:, 0:ntile * 1024]),
                 reads=[WB[("nab", l)]], writes=[bnab], dma=True)
            S.op("sp", lambda e: e.dma_start(out=swQ[:], in_=PT[par][12:16, :, a:b_hi].rearrange("c p n -> p c n")),
                 reads=deps, writes=[bsq], dma=True)
            slo = max(0, a - 128)
            S.op("sp", lambda e: e.dma_start(out=swK[:, slo - (a - 128):Gw + 256], in_=PT[par][16, :, slo:b_hi + 128]),
                 reads=deps, writes=[bsk], dma=True)
            s_lo = max(0, i0 - 1)
            S.op("sp", lambda e: e.dma_start(out=swV[:, s_lo - (i0 - 1):npr + 2, :],
                                             in_=VSW[par][s_lo:i0 + npr + 1].rearrange("c p n -> p c n")),
                 reads=deps, writes=[bsv], dma=True)
            plo = max(0, a - 8)
            if a == 0:
                S.op("pool", lambda e: e.memset(pin[:, :, 0:8], 0.0), writes=[bpin])
                S.op("dve", lambda e: e.memset(cin[:, :, 0:1], 0.0), writes=[bcin])
            S.op("sp", lambda e: e.dma_start(out=pin[:, :, plo - (a - 8):Gw + 24],
                                             in_=PT[par][8:12, :, plo:b_hi + 16].rearrange("c p n -> p c n")),
                 reads=deps, writes=[bpin], dma=True)
            clo = max(0, a - 1)
            S.op("sp", lambda e: e.dma_start(out=cin[:, :, clo - (a - 1):Gw + 2],
                                             in_=PT[par][17:29, :, clo:b_hi + 1].rearrange("c p n -> p c n")),
                 reads=deps, writes=[bcin], dma=True)

            na_its = []
            for i in range(i0, i0 + npr):
                for j in range(4):
                    na_its.append((i, j))

            def na_cfg(i):
                if i == 0:
                    return [0, 1, 2, 3], [5, 6, 7, 8]
                if i == 1:
                    return [0, 1, 2, 3], [9, 10, 11, 12]
                return [i - 2, i - 1, i, i + 1, i + 2], [0, 1, 2, 3, 4]

            def na_S(it):
                i, j = na_its[it]
                clist, tiles = na_cfg(i)
                ncl = len(clist)
                tq = (i - i0) * 128
                r = it % 2
                base = 3 * r
                for hh in range(2):
                    h = 2 * j + hh
                    pb = hh * 64
                    for ci, c in enumerate(clist):
                        t = hh * ncl + ci
                        bank = base + t // 4
                        col = (t % 4) * 128
                        kc = (c - (i0 - 2)) * 128
                        mm(ps[:, bank, col:col + 128], naK[pb:pb + 64, j, kc:kc + 128], naQ[pb:pb + 64, j, tq:tq + 128],
                           True, False, [bk, bq], [psb[bank]])
                        mm(ps[:, bank, col:col + 128], nab[:, tiles[ci], h, :], ident, False, True,
                           [bnab, cbb], [psb[bank]])
                ntl = 2 * ncl
                for bi in range((ntl + 3) // 4):
                    w_ = min(4, ntl - bi * 4) * 128
                    S.op("act", lambda e: e.activation(out=Pna[:, r, bi * 512:bi * 512 + w_], in_=ps[:, base + bi, 0:w_], func=AF.Exp),
                         reads=[psb[base + bi]], writes=[bP[r]])

            def na_rest(it):
                i, j = na_its[it]
                clist, tiles = na_cfg(i)
                ncl = len(clist)
                tq = (i - i0) * 128
                r = it % 2
                odb = 6 + r
                ntl = 2 * ncl
                P4 = Pna[:, r, 0:ntl * 128].rearrange("p (h c q) -> p h c q", h=2, c=ncl)
                for ci in range(ncl):
                    mm(ps[0:64, odb, 256:512].rearrange("p (h q) -> p h q", h=2), ones[:, 0:64], P4[:, :, ci, :],
                       ci == 0, ci == ncl - 1, [bP[r], cbb], [psb[odb]])
                for hh in range(2):
                    h = 2 * j + hh
                    for ci, c in enumerate(clist):
                        mm(ps[0:64, odb, hh * 128:hh * 128 + 128], naV[:, c - (i0 - 2), h * 64:h * 64 + 64],
                           P4[:, hh, ci, :], ci == 0, ci == ncl - 1, [bP[r], bv], [psb[odb]])
                S.op("dve", lambda e: e.reciprocal(out=rec[0:64, r, 0:256], in_=ps[0:64, odb, 256:512]),
                     reads=[psb[odb]], writes=[brec[r]])
                for hh in range(2):
                    S.op("dve", lambda e: e.tensor_tensor(
                        out=yT[hh * 64:hh * 64 + 64, j, tq:tq + 128], in0=ps[0:64, odb, hh * 128:hh * 128 + 128],
                        in1=rec[0:64, r, hh * 128:hh * 128 + 128], op=ALU.mult),
                        reads=[psb[odb], brec[r]], writes=[yTb[j]])

            na_S(0)
            for it in range(len(na_its)):
                if it + 1 < len(na_its):
                    na_S(it + 1)
                na_rest(it)

            sw_its = []
            for n in range(i0, i0 + npr):
                for kv in range(2):
                    sw_its.append((n, kv))

            def sw_S(it):
                n, kv = sw_its[it]
                jl = [-1, 0, 1] if n > 0 else [0, 1]
                tq = (n - i0) * 128
                pb = kv * 64
                base = kv * 3
                for ji, jj in enumerate(jl):
                    kc = (n + jj) * 128 - (a - 128)
                    o3 = ps[:, base + ji, :].rearrange("p (h q) -> p h q", h=4)
                    mm(o3, swK[pb:pb + 64, kc:kc + 128], swQ[pb:pb + 64, :, tq:tq + 128], True, jj == 0,
                       [bsk, bsq], [psb[base + ji]])
                    if jj != 0:
                        mm(o3, maskq[0 if jj < 0 else 1], bcast(ident, 1, 4), False, True, [cbb], [psb[base + ji]])
                    S.op("act", lambda e: e.activation(out=Psw[:, kv, ji, :], in_=ps[:, base + ji, :], func=AF.Exp),
                         reads=[psb[base + ji]], writes=[bPs[kv]])

            def sw_rest(it):
                n, kv = sw_its[it]
                jl = [-1, 0, 1] if n > 0 else [0, 1]
                tq = (n - i0) * 128
                nj = len(jl)
                for ji, jj in enumerate(jl):
                    mm(ps[0:64, 6, :], swV[:, n + jj - (i0 - 1), kv * 64:kv * 64 + 64], Psw[:, kv, ji, :],
                       ji == 0, ji == nj - 1, [bsv, bPs[kv]], [psb[6]])
                for ji, jj in enumerate(jl):
                    mm(ps[0:64, 7, :], ones[:, 0:64], Psw[:, kv, ji, :], ji == 0, False, [cbb, bPs[kv]], [psb[7]])
                mm(ps[0:64, 7, :].rearrange("p (h q) -> p h q", h=4), ones[0:1, 0:64], bcast(sinke[0:1, l * 8 + kv * 4:l * 8 + kv * 4 + 4], 2, 128),
                   False, True, [cbb, sinkb], [psb[7]])
                S.op("dve", lambda e: e.reciprocal(out=rec[0:64, kv, :], in_=ps[0:64, 7, :]),
                     reads=[psb[7]], writes=[brec[kv]])
                for hh in range(4):
                    h = kv * 4 + hh
                    ch = 8 + h // 2
                    po = (h % 2) * 64
                    S.op("dve", lambda e: e.tensor_tensor(
                        out=yT[po:po + 64, ch, tq:tq + 128], in0=ps[0:64, 6, hh * 128:hh * 128 + 128],
                        in1=rec[0:64, kv, hh * 128:hh * 128 + 128], op=ALU.mult),
                        reads=[psb[6], brec[kv]], writes=[yTb[ch]])

            sw_S(0)
            for it in range(len(sw_its)):
                if it + 1 < len(sw_its):
                    sw_S(it + 1)
                sw_rest(it)

            W_ = Gw + 24
            cgo = L * CW
            for g4 in range(4):
                u = pin[:, g4, :]
                A_, B_, C_ = pw[:, 0, :], pw[:, 1, :], pw[:, 2, :]
                S.op("pool", lambda e, u=u: e.tensor_tensor(out=A_[:, 1:W_], in0=u[:, 0:W_ - 1], in1=u[:, 1:W_], op=ALU.add),
                     reads=[bpin], writes=[bpw[0]])
                cur, curb, oth, othb = A_, bpw[0], B_, bpw[1]
                if g4 >= 1:
                    S.op("pool", lambda e, cur=cur, oth=oth: e.tensor_tensor(
                        out=oth[:, 2:W_ - 1], in0=cur[:, 1:W_ - 2], in1=cur[:, 3:W_], op=ALU.add),
                        reads=[curb], writes=[othb])
                    cur, curb, oth, othb = oth, othb, cur, curb
                if g4 >= 2:
                    S.op("pool", lambda e, cur=cur, oth=oth: e.tensor_tensor(
                        out=oth[:, 4:W_ - 3], in0=cur[:, 2:W_ - 5], in1=cur[:, 6:W_ - 1], op=ALU.add),
                        reads=[curb], writes=[othb])
                    cur, curb, oth, othb = oth, othb, cur, curb
                if g4 >= 3:
                    S.op("pool", lambda e, cur=cur, oth=oth: e.tensor_tensor(
                        out=oth[:, 8:W_ - 7], in0=cur[:, 4:W_ - 11], in1=cur[:, 12:W_ - 3], op=ALU.add),
                        reads=[curb], writes=[othb])
                    cur, curb, oth, othb = oth, othb, cur, curb
                S.op("pool", lambda e, cur=cur, g4=g4: e.tensor_scalar(
                    out=C_[:, 0:Gw], in0=cur[:, 8:8 + Gw], scalar1=cf[:, cgo + g4:cgo + g4 + 1], scalar2=None, op0=ALU.mult),
                    reads=[curb, cfb], writes=[bpw[2]])
                S.op("pool", lambda e, cur=cur, oth=oth, g4=g4: e.tensor_scalar(
                    out=oth[:, 0:Gw], in0=cur[:, 9:9 + Gw], scalar1=cf[:, cgo + 4 + g4:cgo + 5 + g4], scalar2=None, op0=ALU.mult),
                    reads=[curb, cfb], writes=[othb])
                S.op("pool", lambda e, oth=oth: e.tensor_tensor(out=C_[:, 0:Gw], in0=C_[:, 0:Gw], in1=oth[:, 0:Gw], op=ALU.add),
                     reads=[othb, bpw[2]], writes=[bpw[2]])
                if a == 0:
                    ta = cf[:, cgo + 8 + g4 * 16:cgo + 8 + g4 * 16 + 16]
                    tb = cf[:, cgo + 72 + g4 * 16:cgo + 72 + g4 * 16 + 16]
                    S.op("pool", lambda e, cur=cur, ta=ta: e.tensor_tensor(out=C_[:, 0:16], in0=cur[:, 8:24], in1=ta, op=ALU.mult),
                         reads=[curb, cfb, bpw[2]], writes=[bpw[2]])
                    S.op("pool", lambda e, cur=cur, tb=tb, oth=oth: e.tensor_tensor(
                        out=oth[:, 0:16], in0=cur[:, 9:25], in1=tb, op=ALU.mult), reads=[curb, cfb], writes=[othb])
                    S.op("pool", lambda e, oth=oth: e.tensor_tensor(out=C_[:, 0:16], in0=C_[:, 0:16], in1=oth[:, 0:16], op=ALU.add),
                         reads=[othb, bpw[2]], writes=[bpw[2]])
                pr = g4 % 2
                S.op("pool", lambda e, u=u, pr=pr: e.tensor_tensor(out=pooled[:, pr, :], in0=C_[:, 0:Gw], in1=u[:, 8:8 + Gw],
                                                                 op=ALU.subtract),
                     reads=[bpw[2], bpin], writes=[bpooled[pr]])
                bank = 4 + pr
                mm(ps[:, bank, 0:Gw], poolw[:, l * 512 + g4 * 128:l * 512 + g4 * 128 + 128], pooled[:, pr, :], True, True,
                   [poolwb, bpooled[pr]], [psb[bank]])
                S.op("act", lambda e, g4=g4, bank=bank: e.activation(
                    out=yT[:, 4 + g4, 0:Gw], in_=ps[:, bank, 0:Gw], func=AF.Copy,
                    scale=cf[:, lcol(l, 160) + g4:lcol(l, 160) + g4 + 1]),
                    reads=[psb[bank], cfb], writes=[yTb[4 + g4]])

            for cc in range(4):
                r = 0
                z, acc = cw_[:, r, 0, :], cw_[:, r, 1, :]
                hch, bch, cch = cin[:, cc, :], cin[:, 4 + cc, :], cin[:, 8 + cc, :]
                wc = lambda tap, cc=cc: cf[:, lcol(l, 164) + tap * 4 + cc:lcol(l, 164) + tap * 4 + cc + 1]
                S.op("dve", lambda e, z=z, hch=hch, cch=cch: e.tensor_tensor(out=z, in0=cch, in1=hch, op=ALU.mult),
                     reads=[bcin], writes=[bcw[r][0]])
                S.op("dve", lambda e, z=z, acc=acc, wc=wc: e.tensor_scalar(
                    out=acc[:, 0:Gw], in0=z[:, 1:Gw + 1], scalar1=wc(1), scalar2=None, op0=ALU.mult),
                    reads=[bcw[r][0], cfb], writes=[bcw[r][1]])
                S.op("dve", lambda e, z=z, acc=acc, wc=wc: e.scalar_tensor_tensor(
                    out=acc[:, 0:Gw], in0=z[:, 0:Gw], scalar=wc(0), in1=acc[:, 0:Gw], op0=ALU.mult, op1=ALU.add),
                    reads=[bcw[r][0], cfb, bcw[r][1]], writes=[bcw[r][1]])
                S.op("dve", lambda e, z=z, acc=acc, wc=wc: e.scalar_tensor_tensor(
                    out=acc[:, 0:Gw], in0=z[:, 2:Gw + 2], scalar=wc(2), in1=acc[:, 0:Gw], op0=ALU.mult, op1=ALU.add),
                    reads=[bcw[r][0], cfb, bcw[r][1]], writes=[bcw[r][1]])
                S.op("dve", lambda e, acc=acc, bch=bch, cc=cc: e.tensor_tensor(
                    out=yT[:, 12 + cc, 0:Gw], in0=acc[:, 0:Gw], in1=bch[:, 1:Gw + 1], op=ALU.mult),
                    reads=[bcw[r][1], bcin], writes=[yTb[12 + cc]])
            if debug and l == 0:
                S.op("sp", lambda e: e.dma_start(out=YD[:, :, a:a + Gw].rearrange("c p n -> p c n"), in_=yT[:, :, 0:Gw]),
                     reads=yTb, writes=[Buf()], dma=True)
            S.op("dve", lambda e: e.memset(rec[0:1, 0, 0:1], 0.0), writes=[ARENA, brec[0]], arena=False)

    def layer_norm(l, which, Gw, tk, skip_sq=False, skip_cast=False):
        sq, sqb, mst, mstb = tk["sq"], tk["sqb"], tk["mst"], tk["mstb"]
        gcol = lcol(l, 64 + which * 32)
        bcol = gcol + 16
        for q in range(4):
            cs = slice(q * 4, q * 4 + 4)
            if not skip_sq:
                S.op("pool", lambda e, cs=cs: e.tensor_tensor(out=sq[:, cs, 0:Gw], in0=xT[:, cs, 0:Gw], in1=xT[:, cs, 0:Gw], op=ALU.mult),
                     reads=xTb[cs], writes=sqb[cs])
            if not skip_cast:
                S.op("act", lambda e, cs=cs: e.activation(out=uT[:, cs, 0:Gw], in_=xT[:, cs, 0:Gw], func=AF.Copy),
                     reads=xTb[cs], writes=uTb[cs])
        for k in range(NCH):
            mm(ps[:, 6, 0:Gw], onesdiv, uT[:, k, 0:Gw], k == 0, k == NCH - 1, [cbb, uTb[k]], [psb[6]])
        for k in range(NCH):
            mm(ps[:, 7, 0:Gw], onesdiv, sq[:, k, 0:Gw], k == 0, k == NCH - 1, [cbb, sqb[k]], [psb[7]])
        m, var, rstd = mst[:, 0, 0:Gw], mst[:, 1, 0:Gw], mst[:, 2, 0:Gw]
        S.op("dve", lambda e: e.tensor_copy(out=m, in_=ps[:, 6, 0:Gw]), reads=[psb[6]], writes=[mstb[0]])
        S.op("dve", lambda e: e.tensor_tensor(out=var, in0=m, in1=m, op=ALU.mult), reads=[mstb[0]], writes=[mstb[1]])
        S.op("dve", lambda e: e.tensor_tensor(out=var, in0=ps[:, 7, 0:Gw], in1=var, op=ALU.subtract),
             reads=[psb[7], mstb[1]], writes=[mstb[1]])
        S.op("dve", lambda e: e.tensor_scalar(out=var, in0=var, scalar1=EPS, scalar2=None, op0=ALU.add),
             reads=[mstb[1]], writes=[mstb[1]])
        S.op("act", lambda e: e.activation(out=var, in_=var, func=AF.Sqrt), reads=[mstb[1]], writes=[mstb[1]])
        S.op("dve", lambda e: e.reciprocal(out=rstd, in_=var), reads=[mstb[1]], writes=[mstb[2]])
        for q in range(4):
            cs = slice(q * 4, q * 4 + 4)
            S.op("dve", lambda e, cs=cs: e.tensor_tensor(out=xT[:, cs, 0:Gw], in0=xT[:, cs, 0:Gw], in1=bcast(m, 1, 4), op=ALU.subtract),
                 reads=[mstb[0]], writes=xTb[cs])
            S.op("pool", lambda e, cs=cs: e.tensor_tensor(out=xT[:, cs, 0:Gw], in0=xT[:, cs, 0:Gw], in1=bcast(rstd, 1, 4), op=ALU.mult),
                 reads=[mstb[2]], writes=xTb[cs])
            for dc in range(q * 4, q * 4 + 4):
                S.op("act", lambda e, dc=dc: e.activation(out=uT[:, dc, 0:Gw], in_=xT[:, dc, 0:Gw], func=AF.Identity,
                                                          scale=cf[:, gcol + dc:gcol + dc + 1], bias=cf[:, bcol + dc:bcol + dc + 1]),
                     reads=[xTb[dc], cfb], writes=[uTb[dc]])
        for q in range(4):
            cs = slice(q * 4, q * 4 + 4)
            S.op("dve", lambda e, cs=cs, q=q: e.tensor_tensor(out=xT[:, cs, 0:Gw], in0=xT[:, cs, 0:Gw],
                                                              in1=bcast(cf[:, gcol + q * 4:gcol + q * 4 + 4], 2, Gw), op=ALU.mult),
                 reads=[cfb] + uTb[cs], writes=xTb[cs])
            S.op("pool", lambda e, cs=cs, q=q: e.tensor_tensor(out=xT[:, cs, 0:Gw], in0=xT[:, cs, 0:Gw],
                                                               in1=bcast(cf[:, bcol + q * 4:bcol + q * 4 + 4], 2, Gw), op=ALU.add),
                 reads=[cfb], writes=xTb[cs])

    def in_proj(lp, g, a, Gw, tk):
        par = lp % 2
        stg, stgb, rt, rtb, cosb_, sinb_, tabb = tk["stg"], tk["stgb"], tk["rt"], tk["rtb"], tk["cos"], tk["sin"], tk["tabb"]
        vst, vstb, vs2, vs2b = tk["vst"], tk["vstb"], tk["vs2"], tk["vs2b"]
        ntt = Gw // 128
        S.op("sp", lambda e: e.dma_start(out=cosb_[:, 0:Gw], in_=cos_in[:, a:a + Gw]), writes=[tabb], dma=True)
        S.op("sp", lambda e: e.dma_start(out=sinb_[:, 0:Gw], in_=sin_in[:, a:a + Gw]), writes=[tabb], dma=True)
        pbuf = PROJ[par][g]
        for cc in range(29):
            slab, slb = w_take("win", lp, cc)
            if cc >= cfg.get("ip_n", 99):
                continue
            if tk.get("p0tick"):
                p0_tick()
            bank = cc % 4
            for k in range(NCH):
                mm(ps[:, bank, 0:Gw], slab[:, k * 128:k * 128 + 128], uT[:, k, 0:Gw], k == 0, k == NCH - 1,
                   [slb, uTb[k]], [psb[bank]])
            so = cc % 4
            sc = 0.125 if (cc < 4 or 12 <= cc < 16) else 1.0
            if 12 <= cc <= 16:
                S.op("act", lambda e, bank=bank, sc=sc: e.activation(out=rt[:, 0, 0:Gw], in_=ps[:, bank, 0:Gw], func=AF.Copy, scale=sc),
                     reads=[psb[bank]], writes=[rtb[0]])
                mm(ps[:, 5, 0:Gw], Rt, rt[:, 0, 0:Gw], True, True, [cbb, rtb[0]], [psb[5]])
                t1, t2 = tk["t1"], tk["t2"]
                S.op("dve", lambda e: e.tensor_tensor(out=t1[:, 0:Gw], in0=rt[:, 0, 0:Gw], in1=cosb_[:, 0:Gw], op=ALU.mult),
                     reads=[rtb[0], tabb], writes=[tk["t1b"]])
                S.op("dve", lambda e: e.tensor_tensor(out=t2[:, 0:Gw], in0=ps[:, 5, 0:Gw], in1=sinb_[:, 0:Gw], op=ALU.mult),
                     reads=[psb[5], tabb], writes=[tk["t2b"]])
                S.op("dve", lambda e, so=so: e.tensor_tensor(out=stg[:, so, 0:Gw], in0=t1[:, 0:Gw], in1=t2[:, 0:Gw], op=ALU.add),
                     reads=[tk["t1b"], tk["t2b"]], writes=[stgb[so]])
            else:
                rr["ev"] += 1
                if rr["ev"] % 2 == 0:
                    S.op("act", lambda e, bank=bank, so=so, sc=sc: e.activation(
                        out=stg[:, so, 0:Gw], in_=ps[:, bank, 0:Gw], func=AF.Copy, scale=sc),
                        reads=[psb[bank]], writes=[stgb[so]])
                else:
                    S.op("dve", lambda e, bank=bank, so=so, sc=sc: e.tensor_scalar(
                        out=stg[:, so, 0:Gw], in0=ps[:, bank, 0:Gw], scalar1=sc, scalar2=None, op0=ALU.mult),
                        reads=[psb[bank]], writes=[stgb[so]])
            S.op("sp", lambda e, cc=cc, so=so: e.dma_start(out=PT[par][cc, :, a:a + Gw], in_=stg[:, so, 0:Gw]),
                 reads=[stgb[so]], writes=[pbuf], dma=True)
        if not cfg.get("ip_v", True):
            for s in range(5):
                w_take("win", lp, 29 + s)
            return
        for s in range(4):
            slab, slb = w_take("win", lp, 29 + s)
            for tt in range(ntt):
                for kk in range(4):
                    k = s * 4 + kk
                    mm(ps[:, 4 + tt, :], uT[:, k, tt * 128:tt * 128 + 128], slab[:, kk * 512:kk * 512 + 512],
                       k == 0, k == NCH - 1, [slb, uTb[k]], [psb[4 + tt]])
        for tt in range(ntt):
            if tt % 2 == 0:
                S.op("act", lambda e, tt=tt: e.activation(out=vst[:, tt, :], in_=ps[:, 4 + tt, :], func=AF.Copy),
                     reads=[psb[4 + tt]], writes=[vstb])
            else:
                S.op("dve", lambda e, tt=tt: e.tensor_copy(out=vst[:, tt, :], in_=ps[:, 4 + tt, :]),
                     reads=[psb[4 + tt]], writes=[vstb])
        S.op("sp", lambda e: e.dma_start(out=VNA[par][a // 128:a // 128 + ntt].rearrange("c p n -> p c n"), in_=vst[:, 0:ntt, :]),
             reads=[vstb], writes=[pbuf], dma=True)
        slab, slb = w_take("win", lp, 33)
        for tt in range(ntt):
            for k in range(NCH):
                mm(ps[:, 0, tt * 128:tt * 128 + 128], uT[:, k, tt * 128:tt * 128 + 128], slab[:, k * 128:k * 128 + 128],
                   k == 0, k == NCH - 1, [slb, uTb[k]], [psb[0]])
        S.op("act", lambda e: e.activation(out=vs2[:, 0:ntt, :], in_=ps[:, 0, 0:ntt * 128].rearrange("p (t n) -> p t n", n=128), func=AF.Copy),
             reads=[psb[0]], writes=[vs2b])
        S.op("sp", lambda e: e.dma_start(out=VSW[par][a // 128:a // 128 + ntt].rearrange("c p n -> p c n"), in_=vs2[:, 0:ntt, :]),
             reads=[vs2b], writes=[pbuf], dma=True)

    def alloc_tok(ms, tag, full):
        def msb(name, shape, dt):
            return ms.enter_context(nc.sbuf_tensor(f"{name}_{tag}", list(shape), dt))
        tk = {}
        tk["stg"] = msb("stg", [128, 4, 512], BF16); tk["stgb"] = [Buf() for _ in range(4)]
        tk["rt"] = msb("rt", [128, 1, 512], BF16); tk["rtb"] = [Buf()]
        tk["cos"] = msb("cos", [128, 512], F32); tk["sin"] = msb("sin", [128, 512], F32); tk["tabb"] = Buf()
        tk["t1"] = msb("t1", [128, 512], F32); tk["t2"] = msb("t2", [128, 512], F32); tk["t1b"] = Buf(); tk["t2b"] = Buf()
        tk["vst"] = msb("vst", [128, 4, 512], BF16); tk["vstb"] = Buf()
        tk["vs2"] = msb("vs2", [128, 4, 128], BF16); tk["vs2b"] = Buf()
        nb_ = 2 if full else 6
        p0_bind(msb("p0f", [128, nb_, 1280], F32), msb("p0b", [128, nb_, 1280], BF16), nb_)
        if full:
            tk["mixT"] = msb("mixT", [128, NCH, 512], BF16); tk["mixb"] = [Buf() for _ in range(NCH)]
            tk["sq"] = msb("hT", [128, NCH, 512], BF16); tk["sqb"] = [Buf() for _ in range(NCH)]
            tk["mst"] = msb("mst", [128, 3, 512], F32); tk["mstb"] = [Buf() for _ in range(3)]
            tk["gt"] = msb("gt", [128, 2, 512], F32); tk["gtb"] = [Buf(), Buf()]
            tk["pd"] = msb("pd", [128, 2, 512], F32); tk["pdb"] = [Buf(), Buf()]
            tk["ac"] = msb("ac", [128, 2, 512], F32); tk["acb"] = [Buf(), Buf()]
            tk["qT"] = msb("qT", [128, 4, 512], BF16); tk["qTb"] = [Buf() for _ in range(4)]
            tk["oT"] = msb("oT", [128, 4, 512], BF16); tk["oTb"] = [Buf() for _ in range(4)]
            tk["Pc"] = msb("Pc", [128, 2, 2, 512], BF16); tk["Pcb"] = [Buf(), Buf()]
            tk["rc"] = msb("rc", [128, 512], F32); tk["rcb"] = Buf()
            tk["rl"] = msb("rl", [128, 2, 512], F32); tk["rlb"] = [Buf(), Buf()]
        return tk

    def t1_load(l, g, a, Gw):
        src = xin if l == 0 else XR
        S.op("sp", lambda e: e.dma_start(out=xT[:, :, 0:Gw], in_=src[:, :, a:a + Gw].rearrange("c p n -> p c n")),
             reads=([] if l == 0 else [XRb[g]]), writes=xTb, dma=True)
        for q in range(4):
            cs = slice(q * 4, q * 4 + 4)
            S.op("act", lambda e, cs=cs: e.activation(out=uT[:, cs, 0:Gw], in_=xT[:, cs, 0:Gw], func=AF.Copy),
                 reads=xTb[cs], writes=uTb[cs])

    def tok_stage(l, g, a, Gw):
        with ExitStack() as ms:
            tk = alloc_tok(ms, f"{l}_{g}", True)
            mixT, mixb, hT, hTb = tk["mixT"], tk["mixb"], tk["sq"], tk["sqb"]
            gt, gtb, pd, pdb, ac, acb = tk["gt"], tk["gtb"], tk["pd"], tk["pdb"], tk["ac"], tk["acb"]

            def ln_prep(dc, do_sq=True):
                if do_sq:
                    S.op("dve", lambda e: e.tensor_tensor(out=hT[:, dc, 0:Gw], in0=xT[:, dc, 0:Gw], in1=xT[:, dc, 0:Gw], op=ALU.mult),
                         reads=[xTb[dc]], writes=[hTb[dc]])
                S.op("act", lambda e: e.activation(out=uT[:, dc, 0:Gw], in_=xT[:, dc, 0:Gw], func=AF.Copy),
                     reads=[xTb[dc]], writes=[uTb[dc]])
            n2 = 0
            for dc in range(NCH):
                for i in range(4):
                    slab, slb = w_take("wgb", l, dc * 4 + i)
                    p0_tick()
                    r = n2 % 2
                    n2 += 1
                    gb_, bb_ = r, 2 + r
                    for k in range(NCH):
                        mm(ps[:, gb_, 0:Gw], slab[:, k * 128:k * 128 + 128], uT[:, k, 0:Gw], k == 0, k == NCH - 1,
                           [slb, uTb[k]], [psb[gb_]])
                    for k in range(4):
                        mm(ps[:, bb_, 0:Gw], slab[:, 2048 + k * 128:2048 + k * 128 + 128], yT[:, i * 4 + k, 0:Gw], k == 0, k == 3,
                           [slb, yTb[i * 4 + k]], [psb[bb_]])
                    bgc = lcol(l, i * 16 + dc)
                    S.op("act", lambda e, r=r, gb_=gb_, bgc=bgc: e.activation(
                        out=gt[:, r, 0:Gw], in_=ps[:, gb_, 0:Gw], func=AF.Sigmoid, bias=cf[:, bgc:bgc + 1]),
                        reads=[psb[gb_], cfb], writes=[gtb[r]])
                    ar = dc % 2
                    if i == 0:
                        S.op("dve", lambda e, r=r, bb_=bb_, ar=ar: e.tensor_tensor(
                            out=ac[:, ar, 0:Gw], in0=ps[:, bb_, 0:Gw], in1=gt[:, r, 0:Gw], op=ALU.mult),
                            reads=[psb[bb_], gtb[r]], writes=[acb[ar]])
                    else:
                        S.op("dve", lambda e, r=r, bb_=bb_: e.tensor_tensor(
                            out=pd[:, r, 0:Gw], in0=ps[:, bb_, 0:Gw], in1=gt[:, r, 0:Gw], op=ALU.mult),
                            reads=[psb[bb_], gtb[r]], writes=[pdb[r]])
                        if i < 3:
                            S.op("pool", lambda e, r=r, ar=ar: e.tensor_tensor(
                                out=ac[:, ar, 0:Gw], in0=ac[:, ar, 0:Gw], in1=pd[:, r, 0:Gw], op=ALU.add),
                                reads=[pdb[r], acb[ar]], writes=[acb[ar]])
                        else:
                            S.op("pool", lambda e, r=r, ar=ar, dc=dc: e.tensor_tensor(
                                out=mixT[:, dc, 0:Gw], in0=ac[:, ar, 0:Gw], in1=pd[:, r, 0:Gw], op=ALU.add),
                                reads=[pdb[r], acb[ar]], writes=[mixb[dc]])
            for dc in range(NCH):
                slab, slb = w_take("wout", l, dc)
                bank = 4 + dc % 2
                for k in range(NCH):
                    mm(ps[:, bank, 0:Gw], slab[:, k * 128:k * 128 + 128], mixT[:, k, 0:Gw], k == 0, k == NCH - 1,
                       [slb, mixb[k]], [psb[bank]])
                S.op("dve", lambda e, dc=dc, bank=bank: e.scalar_tensor_tensor(
                    out=xT[:, dc, 0:Gw], in0=xT[:, dc, 0:Gw], scalar=ALPHA, in1=ps[:, bank, 0:Gw], op0=ALU.mult, op1=ALU.add),
                    reads=[psb[bank]], writes=[xTb[dc]])
                ln_prep(dc)
            layer_norm(l, 0, Gw, tk, skip_sq=True, skip_cast=True)
            qT, qTb, oT, oTb, Pc, Pcb, rc, rcb = tk["qT"], tk["qTb"], tk["oT"], tk["oTb"], tk["Pc"], tk["Pcb"], tk["rc"], tk["rcb"]
            for h in range(4):
                slab, slb = w_take("wq", l, h)
                bank = h % 2
                for k in range(NCH):
                    mm(ps[:, bank, 0:Gw], slab[:, k * 128:k * 128 + 128], uT[:, k, 0:Gw], k == 0, k == NCH - 1,
                       [slb, uTb[k]], [psb[bank]])
                S.op("act", lambda e, h=h, bank=bank: e.activation(out=qT[:, h, 0:Gw], in_=ps[:, bank, 0:Gw], func=AF.Copy),
                     reads=[psb[bank]], writes=[qTb[h]])
            def ca_S(h):
                r = h % 2
                for mc in range(2):
                    bank = 2 + r * 2 + mc
                    mm(ps[:, bank, 0:Gw], KT[:, h, mc * 128:mc * 128 + 128], qT[:, h, 0:Gw], True, True, [KTb, qTb[h]], [psb[bank]])
                    S.op("act", lambda e: e.activation(out=Pc[:, r, mc, 0:Gw], in_=ps[:, bank, 0:Gw], func=AF.Exp, scale=float(128 ** -0.5)),
                         reads=[psb[bank]], writes=[Pcb[r]])

            def ca_rest(h):
                r = h % 2
                for mc in range(2):
                    mm(ps[:, 6, 0:Gw], Vm[:, mc, h * 128:h * 128 + 128], Pc[:, r, mc, 0:Gw], mc == 0, mc == 1, [Vmb, Pcb[r]], [psb[6]])
                for mc in range(2):
                    mm(ps[:, 7, 0:Gw], ones, Pc[:, r, mc, 0:Gw], mc == 0, mc == 1, [cbb, Pcb[r]], [psb[7]])
                S.op("dve", lambda e: e.reciprocal(out=rc[:, 0:Gw], in_=ps[:, 7, 0:Gw]), reads=[psb[7]], writes=[rcb])
                S.op("dve", lambda e: e.tensor_tensor(out=oT[:, h, 0:Gw], in0=ps[:, 6, 0:Gw], in1=rc[:, 0:Gw], op=ALU.mult),
                     reads=[psb[6], rcb], writes=[oTb[h]])

            ca_S(0)
            for h in range(4):
                if h + 1 < 4:
                    ca_S(h + 1)
                ca_rest(h)
            for dg in range(4):
                slab, slb = w_take("wo", l, dg)
                for dcc in range(4):
                    dc = dg * 4 + dcc
                    bank = dc % 2
                    for h in range(4):
                        mm(ps[:, bank, 0:Gw], slab[:, (dcc * 4 + h) * 128:(dcc * 4 + h) * 128 + 128], oT[:, h, 0:Gw], h == 0, h == 3,
                           [slb, oTb[h]], [psb[bank]])
                    S.op("dve", lambda e, dc=dc, bank=bank: e.scalar_tensor_tensor(
                        out=xT[:, dc, 0:Gw], in0=xT[:, dc, 0:Gw], scalar=ALPHA, in1=ps[:, bank, 0:Gw], op0=ALU.mult, op1=ALU.add),
                        reads=[psb[bank]], writes=[xTb[dc]])
                    ln_prep(dc)
            layer_norm(l, 1, Gw, tk, skip_sq=True, skip_cast=True)
            rl, rlb = tk["rl"], tk["rlb"]
            n5 = 0
            for qq in range(4):
                for fcc in range(16):
                    slab, slb = w_take("w1", l, qq * 16 + fcc)
                    bank = n5 % 4
                    r = n5 % 2
                    n5 += 1
                    for k in range(NCH):
                        mm(ps[:, bank, 0:Gw], slab[:, k * 128:k * 128 + 128], uT[:, k, 0:Gw], k == 0, k == NCH - 1,
                           [slb, uTb[k]], [psb[bank]])
                    S.op("act", lambda e, r=r, bank=bank: e.activation(out=rl[:, r, 0:Gw], in_=ps[:, bank, 0:Gw], func=AF.Relu),
                         reads=[psb[bank]], writes=[rlb[r]])
                    S.op("pool", lambda e, r=r, fcc=fcc: e.tensor_tensor(out=hT[:, fcc, 0:Gw], in0=rl[:, r, 0:Gw], in1=rl[:, r, 0:Gw], op=ALU.mult),
                         reads=[rlb[r]], writes=[hTb[fcc]])
                for dc in range(NCH):
                    slab, slb = w_take("w2", l, qq * 16 + dc)
                    bank = 4 + dc % 4
                    for k in range(16):
                        mm(ps[:, bank, 0:Gw], slab[:, k * 128:k * 128 + 128], hT[:, k, 0:Gw], k == 0, k == 15,
                           [slb, hTb[k]], [psb[bank]])
                    if qq == 0:
                        S.op("dve", lambda e, dc=dc, bank=bank: e.scalar_tensor_tensor(
                            out=xT[:, dc, 0:Gw], in0=xT[:, dc, 0:Gw], scalar=ALPHA, in1=ps[:, bank, 0:Gw], op0=ALU.mult, op1=ALU.add),
                            reads=[psb[bank]], writes=[xTb[dc]])
                    else:
                        S.op("dve", lambda e, dc=dc, bank=bank: e.tensor_tensor(
                            out=xT[:, dc, 0:Gw], in0=xT[:, dc, 0:Gw], in1=ps[:, bank, 0:Gw], op=ALU.add),
                            reads=[psb[bank]], writes=[xTb[dc]])
                        if qq == 3:
                            ln_prep(dc, do_sq=False)
            layer_norm(l, 2, Gw, tk, skip_cast=True)
            if l + 1 < L:
                S.op("sp", lambda e: e.dma_start(out=XR[:, :, a:a + Gw].rearrange("c p n -> p c n"), in_=xT[:, :, 0:Gw]),
                     reads=xTb, writes=[XRb[g]], dma=True)
                in_proj(l + 1, g, a, Gw, tk)
            else:
                hi = min(a + Gw, 4096)
                if hi > a:
                    S.op("sp", lambda e: e.dma_start(out=outT[:, :, a:hi].rearrange("c p n -> p c n"), in_=xT[:, :, 0:hi - a]),
                         reads=xTb, writes=[OUTb], dma=True)
            p0_tick(flush=True)
            S.op("dve", lambda e: e.memset(tk["rc"][0:1, 0:1], 0.0), writes=[ARENA, tk["rcb"]], arena=False)

    S.arena = ARENA
    p0_phase(mark_after_layer(0), 29 * min(len(groups_of(TLOC)), cfg.get("npre", 99)))
    for g, (a, Gw) in enumerate(groups_of(TLOC)):
        if g >= cfg.get("npre", 99):
            for _ in range(34):
                wstate["taken"] += 1
                wstate["issued"] = max(wstate["issued"], wstate["taken"])
            continue
        with ExitStack() as ms:
            tk = alloc_tok(ms, f"pre_{g}", False)
            S.op("sp", lambda e: e.dma_start(out=xT[:, :, 0:Gw], in_=xin[:, :, a:a + Gw].rearrange("c p n -> p c n")),
                 writes=xTb, dma=True)
            for q in range(4):
                cs = slice(q * 4, q * 4 + 4)
                S.op("act" if q % 2 == 0 else "dve",
                     (lambda e, cs=cs: e.activation(out=uT[:, cs, 0:Gw], in_=xT[:, cs, 0:Gw], func=AF.Copy)) if q % 2 == 0 else
                     (lambda e, cs=cs: e.tensor_copy(out=uT[:, cs, 0:Gw], in_=xT[:, cs, 0:Gw])),
                     reads=xTb[cs], writes=uTb[cs])
            tk["p0tick"] = True
            in_proj(0, g, a, Gw, tk)
            p0_tick(flush=True)
            S.op("dve", lambda e: e.memset(tk["t1"][0:1, 0:1], 0.0), writes=[ARENA, tk["t1b"]], arena=False)
    for l in range(L):
      with nc.sbuf_tensor(f"memb{l}", [128, NCH, 256], BF16) as memb:
       if not cfg.get("kv", True):
        for _ in range(8):
            wstate["taken"] += 1
            wstate["issued"] = max(wstate["issued"], wstate["taken"])
       else:
        S.op("sp", lambda e: e.dma_start(out=memb[:], in_=memb_b), reads=[membdb], writes=[membb], dma=True)
        for h in range(4):
            slab, slb = w_take("wkvk", l, h)
            bank = h % 2
            for k in range(NCH):
                mm(ps[:, bank, 0:256], slab[:, k * 128:k * 128 + 128], memb[:, k, :], k == 0, k == NCH - 1, [slb, membb], [psb[bank]])
            S.op("act", lambda e, h=h, bank=bank: e.activation(out=KT[:, h, :], in_=ps[:, bank, 0:256], func=AF.Copy),
                 reads=[psb[bank]], writes=[KTb])
        for s in range(4):
            slab, slb = w_take("wkvv", l, s)
            for mc in range(2):
                for kk in range(4):
                    k = s * 4 + kk
                    mm(ps[:, 4 + mc, :], memb[:, k, mc * 128:mc * 128 + 128], slab[:, kk * 512:kk * 512 + 512], k == 0, k == NCH - 1,
                       [slb, membb], [psb[4 + mc]])
        for mc in range(2):
            S.op("act", lambda e, mc=mc: e.activation(out=Vm[:, mc, :], in_=ps[:, 4 + mc, :], func=AF.Copy),
                 reads=[psb[4 + mc]], writes=[Vmb])
        S.op("dve", lambda e: e.memset(memb[0:1, 0, 0:1], 0.0), writes=[ARENA, membb], arena=False)
      p0_phase(mark_after_layer(l + 1), 64 * min(len(groups_of(VL[l])), cfg.get("ngrp", 99)))
      for g, (a, Gw) in enumerate(groups_of(VL[l])):
            if g >= cfg.get("ngrp", 99):
                continue
            if cfg.get("tok", True):
                t1_load(l, g, a, Gw)
            if cfg.get("mix", True):
                mixers(l, g, a, Gw)
            if cfg.get("tok", True):
                tok_stage(l, g, a, Gw)
    if not cfg:
        assert wstate["taken"] == len(seq), (wstate, len(seq))
    S.finish()
    st.close()
    return nc, S


def slab_k(W):
    K, N = W.shape
    return np.ascontiguousarray(W.reshape(K // 128, 128, N // 128, 128).transpose(2, 1, 0, 3)).reshape(N // 128, 128, K)


def prep_weights(inp, L):
    out = {k: [] for k in KINDS}
    for l in range(L):
        g = lambda n: np.asarray(inp[n][l], dtype=np.float32)
        wg, wbr = g("w_gate"), g("w_branch")
        sg = [slab_k(wg[i]) for i in range(4)]
        sbr = [slab_k(wbr[i]) for i in range(4)]
        wgb = np.empty((64, 128, 2560), np.float32)
        for dc in range(16):
            for i in range(4):
                wgb[dc * 4 + i, :, 0:2048] = sg[i][dc]
                wgb[dc * 4 + i, :, 2048:2560] = sbr[i][dc]
        out["wgb"].append(wgb)
        out["wout"].append(slab_k(g("w_mix_out")))
        out["wq"].append(slab_k(g("wq_mem")))
        wkv = g("wkv_mem")
        out["wkvk"].append(slab_k(wkv[:, 0:512]))
        out["wkvv"].append(np.ascontiguousarray(wkv[:, 512:1024].reshape(4, 4, 128, 512).transpose(0, 2, 1, 3)).reshape(4, 128, 2048))
        wo = g("wo_mem")
        out["wo"].append(np.ascontiguousarray(wo.reshape(4, 128, 4, 4, 128).transpose(2, 1, 3, 0, 4)).reshape(4, 128, 2048))
        out["w1"].append(slab_k(g("w_ff1")))
        w2 = g("w_ff2")
        out["w2"].append(np.ascontiguousarray(w2.reshape(4, 16, 128, 16, 128).transpose(0, 3, 2, 1, 4)).reshape(64, 128, 2048))
        win = g("w_in")
        cols = list(range(0, 1024)) + list(range(1536, 2048))
        for c in range(4):
            cols += list(range(2048 + c * 64, 2048 + c * 64 + 64)) + list(range(2048 + (c + 4) * 64, 2048 + (c + 4) * 64 + 64))
        cols += list(range(2560, 2688)) + list(range(2816, 4352))
        fm = slab_k(win[:, np.array(cols)])
        vna = np.ascontiguousarray(win[:, 1024:1536].reshape(4, 4, 128, 512).transpose(0, 2, 1, 3)).reshape(4, 128, 2048)
        vsw = np.ascontiguousarray(win[:, 2688:2816].reshape(16, 128, 128).transpose(1, 0, 2)).reshape(1, 128, 2048)
        out["win"].append(np.concatenate([fm, vna, vsw], axis=0))
    return {k + "_f": np.stack(v) for k, v in out.items()}


def na_tile(rpb, typ, i, c):
    idx = np.arange(128)
    qr, qc = idx // 64, idx % 64
    Rl, KRl = 2 * i + qr, 2 * c + qr
    if typ == 0:
        R, KR, QC, KC = Rl, KRl, qc, qc
    else:
        R, KR, QC, KC = 127 - Rl, 127 - KRl, 63 - qc, 63 - qc
    rs = np.clip(R - 4, 0, 120)
    row_ok = (KR[None, :] >= rs[:, None]) & (KR[None, :] < rs[:, None] + 8)
    ws = np.clip(QC - 8, 0, 48)
    col_ok = (KC[None, :] >= ws[:, None]) & (KC[None, :] < ws[:, None] + 16)
    drow = np.clip(KR[None, :] - R[:, None] + 7, 0, 14)
    dcol = np.clip(KC[None, :] - QC[:, None], -15, 15) + 15
    val = rpb[:, drow, dcol]
    return np.where((row_ok & col_ok)[None], val, np.float32(NEG)).astype(np.float32)


def prep_core_tables(inp, L, typ):
    nab = np.empty((L, 128, 13, 8, 128), np.float32)
    for l in range(L):
        rpb = np.asarray(inp["na_rpb"][l], np.float32)
        tl = [na_tile(rpb, typ, 10, 10 + j) for j in (-2, -1, 0, 1, 2)]
        tl += [na_tile(rpb, typ, 0, c) for c in range(4)] + [na_tile(rpb, typ, 1, c) for c in range(4)]
        for t, tile in enumerate(tl):
            nab[l, :, t, :, :] = tile.transpose(1, 0, 2)
    pos = np.arange(TLOC, dtype=np.float32) if typ == 0 else (np.float32(T - 1) - np.arange(TLOC, dtype=np.float32))
    half = 8
    inv = (np.float32(500000.0) ** (-np.arange(half, dtype=np.float32) / np.float32(half))).astype(np.float32)
    ang = (pos[None, :] * inv[:, None]).astype(np.float32)
    cosT = np.ones((128, TLOC), np.float32)
    sinT = np.zeros((128, TLOC), np.float32)
    for blk in range(2):
        for d in range(16):
            cosT[blk * 64 + d] = np.cos(ang[d % 8])
            sinT[blk * 64 + d] = np.sin(ang[d % 8])
    cfa = np.zeros((128, L * CW + CG), np.float32)
    for l in range(L):
        o = l * CW
        bg = np.asarray(inp["b_gate"][l], np.float32)
        for i in range(4):
            cfa[:, o + i * 16:o + i * 16 + 16] = bg[i].reshape(16, 128).T
        for w, (gn, bn) in enumerate((("ln1_g", "ln1_b"), ("ln2_g", "ln2_b"), ("ln3_g", "ln3_b"))):
            cfa[:, o + 64 + w * 32:o + 64 + w * 32 + 16] = np.asarray(inp[gn][l], np.float32).reshape(16, 128).T
            cfa[:, o + 64 + w * 32 + 16:o + 64 + w * 32 + 32] = np.asarray(inp[bn][l], np.float32).reshape(16, 128).T
        cfa[:, o + 160:o + 164] = np.asarray(inp["pool_scale"][l], np.float32).reshape(4, 128).T
        cw = np.asarray(inp["conv_w"][l], np.float32)
        if typ == 1:
            cw = cw[::-1]
        for tap in range(3):
            cfa[:, o + 164 + tap * 4:o + 164 + tap * 4 + 4] = cw[tap].reshape(4, 128).T
    o = L * CW
    p16 = np.arange(16)
    for g4 in range(4):
        w = 2 ** (g4 + 1)
        h2 = w // 2
        if typ == 0:
            cfa[:, o + g4] = 1.0 / w
            cfa[:, o + 8 + g4 * 16:o + 8 + g4 * 16 + 16] = (1.0 / np.minimum(w, p16 + h2))[None, :]
        else:
            cfa[:, o + 4 + g4] = 1.0 / w
            cfa[:, o + 72 + g4 * 16:o + 72 + g4 * 16 + 16] = (1.0 / np.minimum(w, h2 + p16 + 1))[None, :]
    return {"nab_f": nab.reshape(L, 128, 13 * 8 * 128), "cosT": cosT, "sinT": sinT, "cf": cfa}


def prep_shared(inp, L):
    cbf = np.zeros((128, 768), np.float32)
    cbf[:, 0:128] = np.eye(128, dtype=np.float32)
    Rm = np.zeros((128, 128), np.float32)
    for blk in range(2):
        for d in range(8):
            Rm[blk * 64 + d, blk * 64 + d + 8] = -1.0
            Rm[blk * 64 + d + 8, blk * 64 + d] = 1.0
    cbf[:, 128:256] = Rm.T
    q = np.arange(128)[:, None]
    k = np.arange(128)[None, :]
    cbf[:, 256:384] = np.where(k >= q, 0.0, NEG)
    cbf[:, 384:512] = np.where(k <= q, 0.0, NEG)
    cbf[:, 512:640] = 1.0
    cbf[:, 640:768] = 1.0 / 2048.0
    poolw = np.concatenate([np.asarray(inp["pool_w"][l], np.float32).transpose(1, 0, 2).reshape(128, 512) for l in range(L)], axis=1)
    sink = np.asarray(inp["swa_sinks"][:L], np.float32).reshape(1, L * 8)
    return {"cb_f": cbf, "poolw_f": np.ascontiguousarray(poolw), "sinkrow": np.ascontiguousarray(sink)}


def core_tokens(typ):
    return np.arange(TLOC) if typ == 0 else (T - 1 - np.arange(TLOC))


_CACHE = {}


def make_in_maps(inp, L, cores):
    shared = prep_weights(inp, L)
    shared.update(prep_shared(inp, L))
    tabs = [prep_core_tables(inp, L, t) for t in range(2)]
    maps = []
    for c in cores:
        b, typ = c // 2, c % 2
        idx = core_tokens(typ)
        m = dict(shared)
        m.update(tabs[typ])
        m["xin"] = np.ascontiguousarray(np.asarray(inp["x"][b], np.float32)[idx, :].T).reshape(NCH, 128, TLOC)
        m["memT"] = np.ascontiguousarray(np.asarray(inp["mem"][b], np.float32).T).reshape(NCH, 128, 256)
        maps.append(m)
    return maps


def kernel(**inputs):
    if "nc" not in _CACHE:
        _CACHE["nc"] = build_program(DEPTH)[0]
    nc = _CACHE["nc"]
    cores = list(range(8))
    maps = make_in_maps(inputs, DEPTH, cores)
    res = run_bass_kernel_spmd(nc, maps, core_ids=cores)
    out = np.empty((4, T, D), np.float32)
    for c in cores:
        b, typ = c // 2, c % 2
        idx = core_tokens(typ)[:4096]
        out[b, idx, :] = res.results[c]["outT"].reshape(D, 4096).T
    return out
```

```python
import numpy as np
from contextlib import ExitStack
import concourse.bass as bass
import concourse.mybir as mybir
from concourse.bass_utils import run_bass_kernel_spmd

F32 = mybir.dt.float32
BF16 = mybir.dt.bfloat16
AF = mybir.ActivationFunctionType
ALU = mybir.AluOpType

D = 2048
NCH = 16
T = 8192
TLOC = 5120
DEPTH = 4
VL = [4864, 4608, 4352, 4096]
ALPHA = float(8 ** 0.25)
EPS = 1e-5
NEG = -30000.0
CW = 176
CG = 136
WCOLS = 2560
NSLOT = 5
NDS = 16

KINDS = {
    "wgb": (64, 2560), "wout": (16, 2048), "wq": (4, 2048), "wkvk": (4, 2048), "wkvv": (4, 2048),
    "wo": (4, 2048), "w1": (64, 2048), "w2": (64, 2048), "win": (34, 2048),
}


class Buf:
    __slots__ = ("name", "w", "r")

    def __init__(self, name=""):
        self.name = name
        self.w = None
        self.r = {}


class Sched:
    LIMIT = 30000

    def __init__(self, nc):
        self.nc = nc
        self.E = {"pe": nc.tensor, "act": nc.scalar, "dve": nc.vector, "pool": nc.gpsimd, "sp": nc.sync}
        self.sems = []
        self.semidx = {}
        self.cnt = {}
        self.pe_sems = set()
        for e in ("pe", "act", "dve", "pool"):
            self._new_epoch(e)
        self.dsi = []
        for i in range(NDS):
            self.sems.append(nc.alloc_semaphore(name=f"dq{i}"))
            self.dsi.append(len(self.sems) - 1)
        self.dgen = [0] * NDS
        self.dnext = 0
        self.waited = {e: {} for e in self.E}
        self.arena = None
        self.nops = 0
        self.pool_to_dve = False

    def _new_epoch(self, e):
        self.sems.append(self.nc.alloc_semaphore(name=f"s_{e}_{len(self.sems)}"))
        self.semidx[e] = len(self.sems) - 1
        self.cnt[e] = 0
        if e == "pe":
            self.pe_sems.add(self.semidx[e])

    def new_sem(self, name):
        self.sems.append(self.nc.alloc_semaphore(name=name))
        return len(self.sems) - 1

    def op(self, eng, fn, reads=(), writes=(), dma=False, arena=True, sem=None):
        if eng == "pool!":
            eng = "pool"
        elif eng == "pool" and self.pool_to_dve:
            eng = "dve"
        deps = {}

        def add(tok):
            if tok is None:
                return
            s, v = tok
            if deps.get(s, 0) < v:
                deps[s] = v

        rs = list(reads)
        if arena and self.arena is not None:
            rs.append(self.arena)
        for b in rs:
            add(b.w)
        for b in writes:
            add(b.w)
            for s, v in b.r.items():
                add((s, v))
        j = None
        if dma and sem is None:
            j = self.dnext
            self.dnext = (j + 1) % NDS
            if self.dgen[j] > 0:
                add((self.dsi[j], 16 * self.dgen[j]))
        wt = self.waited[eng]
        E = self.E[eng]
        for s, v in deps.items():
            if eng == "pe" and s in self.pe_sems:
                continue
            if wt.get(s, 0) >= v:
                continue
            E.wait_ge(self.sems[s], v)
            wt[s] = v
        ins = fn(E)
        if dma:
            if sem is not None:
                si, val = sem
                tok = (si, val)
            else:
                self.dgen[j] += 1
                tok = (self.dsi[j], 16 * self.dgen[j])
            ins.then_inc(self.sems[tok[0]], 16)
        else:
            self.cnt[eng] += 1
            tok = (self.semidx[eng], self.cnt[eng])
            ins.then_inc(self.sems[tok[0]], 1)
            if self.cnt[eng] >= self.LIMIT:
                self._new_epoch(eng)
        for b in rs:
            if b.r.get(tok[0], 0) < tok[1]:
                b.r[tok[0]] = tok[1]
        for b in writes:
            b.w = tok
            b.r = {}
        self.nops += 1
        return tok

    def finish(self):
        E = self.E["sp"]
        for j in range(NDS):
            if self.dgen[j] > 0:
                E.wait_ge(self.sems[self.dsi[j]], 16 * self.dgen[j])
        for e in ("pe", "act", "dve", "pool"):
            if self.cnt[e] > 0:
                E.wait_ge(self.sems[self.semidx[e]], self.cnt[e])


def bcast(ap, axis, n):
    dims = [list(d) for d in ap.ap]
    dims.insert(axis, [0, n])
    return bass.AP(ap.tensor, ap.offset, dims)


def groups_of(v):
    gs = []
    a = 0
    while a < v:
        w = min(512, v - a)
        gs.append((a, w))
        a += w
    return gs


def build_program(L=DEPTH, debug=False, cfg=None):
    cfg = cfg or {}
    nc = bass.Bass("TRN2", target_bir_lowering=False)
    S = Sched(nc)
    st = ExitStack()

    def din(name, shape, dt=F32):
        return nc.dram_tensor(name, list(shape), dt, kind="ExternalInput").ap()

    def dscr(name, shape, dt, out=False):
        return nc.dram_tensor(name, list(shape), dt, kind=("ExternalOutput" if out else "Internal")).ap()

    def sb(name, shape, dt):
        return st.enter_context(nc.sbuf_tensor(name, list(shape), dt))

    xin = din("xin", [NCH, 128, TLOC])
    memT = din("memT", [NCH, 128, 256])
    wf = {k: din(k + "_f", [L, n, 128, c]) for k, (n, c) in KINDS.items()}
    wb = {k: dscr(k + "_b", [L, n, 128, c], BF16) for k, (n, c) in KINDS.items()}
    nab_f = din("nab_f", [L, 128, 13 * 8 * 128])
    nab_b = dscr("nab_b", [L, 128, 13 * 8 * 128], BF16)
    poolw_f = din("poolw_f", [128, L * 512])
    cf_in = din("cf", [128, L * CW + CG])
    cb_in = din("cb_f", [128, 768])
    sink_in = din("sinkrow", [1, L * 8])
    cos_in = din("cosT", [128, TLOC])
    sin_in = din("sinT", [128, TLOC])
    PT = [dscr(f"PT{p}", [29, 128, TLOC], BF16, out=debug) for p in range(2)]
    VNA = [dscr(f"VNA{p}", [TLOC // 128, 128, 512], BF16, out=debug) for p in range(2)]
    VSW = [dscr(f"VSW{p}", [TLOC // 128, 128, 128], BF16, out=debug) for p in range(2)]
    XR = dscr("XR", [NCH, 128, VL[0]], F32, out=debug)
    YD = dscr("YD", [NCH, 128, VL[0]], BF16, out=True) if debug else None
    outT = dscr("outT", [NCH, 128, 4096], F32, out=True)
    PROJ = [[Buf(f"proj{p}_{g}") for g in range(10)] for p in range(2)]
    XRb = [Buf(f"xr{g}") for g in range(10)]
    OUTb = Buf("out")

    xT = sb("xT", [128, NCH, 512], F32)
    uT = sb("uT", [128, NCH, 512], BF16)
    yT = sb("yT", [128, NCH, 512], BF16)
    wsl = sb("wsl", [128, NSLOT, WCOLS], BF16)
    cf = sb("cfs", [128, L * CW + CG], F32)
    cb = sb("cbs", [128, 768], BF16)
    poolw = sb("poolw", [128, L * 512], BF16)
    KT = sb("KT", [128, 4, 256], BF16)
    Vm = sb("Vm", [128, 2, 512], BF16)
    sinkf = sb("sinkf", [1, L * 8], F32)
    sinke = sb("sinke", [1, L * 8], BF16)
    ps = st.enter_context(nc.psum_tensor("ps", [128, 8, 512], F32))
    xTb = [Buf(f"xT{i}") for i in range(NCH)]
    uTb = [Buf(f"uT{i}") for i in range(NCH)]
    yTb = [Buf(f"yT{i}") for i in range(NCH)]
    wslb = [Buf(f"ws{i}") for i in range(NSLOT)]
    psb = [Buf(f"ps{i}") for i in range(8)]
    cfb, cbb, poolwb, membb, KTb, Vmb, sinkb = (Buf("cf"), Buf("cb"), Buf("poolw"), Buf("memb"), Buf("KT"),
                                                Buf("Vm"), Buf("sink"))
    ARENA = Buf("arena")

    ident = cb[:, 0:128]
    Rt = cb[:, 128:256]
    maskq = [cb[:, 256:384], cb[:, 384:512]]
    ones = cb[:, 512:640]
    onesdiv = cb[:, 640:768]

    S.arena = ARENA
    S.op("sp", lambda e: e.dma_start(out=cf[:], in_=cf_in), writes=[cfb], dma=True)
    S.op("sp", lambda e: e.dma_start(out=sinkf[:], in_=sink_in), writes=[sinkb], dma=True)
    S.op("act", lambda e: e.activation(out=sinke[:], in_=sinkf[:], func=AF.Exp), reads=[sinkb], writes=[sinkb])
    memb_b = dscr("memb_b", [128, NCH, 256], BF16)
    membdb = Buf("membd")
    WB = {}
    order = []
    for l in range(L):
        ks = ["win"] if l == 0 else []
        ks += ["wkvk", "wkvv", "wgb", "wout", "wq", "wo", "w1", "w2"]
        order += [(k, l) for k in ks]
        order.append(("nab", l))
        if l + 1 < L:
            order.append(("win", l + 1))
    jobs = []
    jobs.append((cb_in, 768, cb[:], None, "cb"))
    jobs.append((poolw_f[:, 0:L * 512 // 2], L * 512 // 2, poolw[:, 0:L * 512 // 2], None, "poolw"))
    jobs.append((poolw_f[:, L * 512 // 2:], L * 512 // 2, poolw[:, L * 512 // 2:], None, "poolw"))
    for hf in range(4):
        jobs.append((memT[hf * 4:hf * 4 + 4].rearrange("c p n -> p c n"), 1024, None, memb_b[:, hf * 4:hf * 4 + 4, :], "memb"))
    marks = {}
    for (k, l) in order:
        if k == "nab":
            for c0 in range(0, 13 * 1024, 1280):
                c1 = min(13 * 1024, c0 + 1280)
                jobs.append((nab_f[l][:, c0:c1], c1 - c0, None, nab_b[l][:, c0:c1], (k, l)))
        else:
            n, c = KINDS[k]
            h = c // 2
            for idx in range(n):
                for hh in range(2):
                    jobs.append((wf[k][l, idx][:, hh * h:(hh + 1) * h], h, None, wb[k][l, idx][:, hh * h:(hh + 1) * h], (k, l)))
        marks[(k, l)] = len(jobs)
    for (k, l) in order:
        WB[(k, l)] = Buf(f"{k}{l}")
    keysem = {}
    keycnt = {}
    keytot = {}
    for j in jobs:
        if j[3] is not None:
            keytot[j[4]] = keytot.get(j[4], 0) + 1
    for key in keytot:
        keysem[key] = S.new_sem(f"c_{key}")
        keycnt[key] = 0
        tok = (keysem[key], 16 * keytot[key])
        if key == "memb":
            membdb.w = tok
        else:
            WB[key].w = tok
    p0s = {"ld": 0, "cs": 0, "st": 0, "stf": None}

    def p0_bind(stf, stbf, nbuf=2):
        p0s["stf"], p0s["stb"] = stf, stbf
        p0s["nb"] = nbuf
        p0s["fb"] = [Buf() for _ in range(nbuf)]
        p0s["bb"] = [Buf() for _ in range(nbuf)]
        p0s["base"] = p0s["cs"]
        assert p0s["ld"] == p0s["cs"] == p0s["st"]

    def p0_load(n):
        src, c, dst, store, key = jobs[n]
        r = n % p0s["nb"]
        stf = p0s["stf"]
        if len(src.shape) == 3:
            o = stf[:, r, 0:c].rearrange("p (a b) -> p a b", a=src.shape[1])
        else:
            o = stf[:, r, 0:c]
        S.op("sp", lambda e: e.dma_start(out=o, in_=src), writes=[p0s["fb"][r]], dma=True)

    def p0_cast(n):
        src, c, dst, store, key = jobs[n]
        r = n % p0s["nb"]
        stf, stbf = p0s["stf"], p0s["stb"]
        eng = "act" if n % 2 == 0 else "dve"
        if p0s.get("use_pool") and n % 3 != 2:
            eng = "pool!"
        if dst is not None:
            o, wr = dst, [{"cb": cbb, "poolw": poolwb}[key]]
        else:
            o, wr = stbf[:, r, 0:c], [p0s["bb"][r]]
        if eng == "act":
            S.op(eng, lambda e: e.activation(out=o, in_=stf[:, r, 0:c], func=AF.Copy), reads=[p0s["fb"][r]], writes=wr)
        else:
            S.op(eng, lambda e: e.tensor_copy(out=o, in_=stf[:, r, 0:c]), reads=[p0s["fb"][r]], writes=wr)

    def p0_store(n):
        src, c, dst, store, key = jobs[n]
        if store is None:
            return
        r = n % p0s["nb"]
        stbf = p0s["stb"]
        keycnt[key] += 1
        if len(store.shape) == 3:
            i_ = stbf[:, r, 0:c].rearrange("p (a b) -> p a b", a=store.shape[1])
        else:
            i_ = stbf[:, r, 0:c]
        S.op("sp", lambda e: e.dma_start(out=store, in_=i_), reads=[p0s["bb"][r]], dma=True, sem=(keysem[key], 16 * keycnt[key]))

    def p0_advance(target, flush=False):
        target = min(target, len(jobs))
        while p0s["cs"] < target:
            n = p0s["cs"]
            while p0s["ld"] < min(target, n + p0s["nb"]) or p0s["ld"] <= n:
                p0_load(p0s["ld"])
                p0s["ld"] += 1
            p0_cast(n)
            p0s["cs"] = n + 1
            if p0s["st"] < n:
                p0_store(p0s["st"])
                p0s["st"] += 1
        if flush:
            while p0s["st"] < p0s["cs"]:
                p0_store(p0s["st"])
                p0s["st"] += 1

    n_up = marks[("win", 0)] if cfg.get("p0", True) else 7
    if not cfg.get("p0", True):
        jobs = jobs[:7]
    with ExitStack() as p0:
        stf0 = p0.enter_context(nc.sbuf_tensor("p0f", [128, 6, 1280], F32))
        stb0 = p0.enter_context(nc.sbuf_tensor("p0b", [128, 6, 1280], BF16))
        p0_bind(stf0, stb0, 6)
        p0_advance(n_up, flush=True)
        S.op("dve", lambda e: e.memset(stf0[0:1, 0, 0:1], 0.0), writes=[ARENA, p0s["fb"][0]], arena=False)
    def mark_after_layer(l):
        if ("win", l + 1) in marks:
            return marks[("win", l + 1)]
        return len(jobs)
    sched_p0 = {"lo": n_up, "hi": n_up, "steps": 1, "i": 0}

    def p0_phase(hi, steps):
        sched_p0["lo"] = p0s["cs"]
        sched_p0["hi"] = min(hi, len(jobs))
        sched_p0["steps"] = max(1, steps)
        sched_p0["i"] = 0

    def p0_tick(flush=False):
        sched_p0["i"] += 1
        f = min(1.0, sched_p0["i"] / sched_p0["steps"])
        tgt = sched_p0["lo"] + int(round(f * (sched_p0["hi"] - sched_p0["lo"])))
        if tgt > p0s["cs"] or flush:
            p0_advance(max(tgt, p0s["cs"]), flush=flush)

    S.pool_to_dve = True
    seq = []

    def seq_group(l, tok, proj):
        s = []
        if tok:
            s += [("wgb", l, i) for i in range(64)]
            s += [("wout", l, i) for i in range(16)]
            s += [("wq", l, i) for i in range(4)]
            s += [("wo", l, i) for i in range(4)]
            for qq in range(4):
                s += [("w1", l, qq * 16 + i) for i in range(16)]
                s += [("w2", l, qq * 16 + i) for i in range(16)]
        if proj:
            s += [("win", l + 1 if tok else l, i) for i in range(34)]
        return s

    for _ in groups_of(TLOC):
        seq += seq_group(0, False, True)
    for l in range(L):
        seq += [("wkvk", l, i) for i in range(4)] + [("wkvv", l, i) for i in range(4)]
        for _ in groups_of(VL[l]):
            seq += seq_group(l, True, l + 1 < L)
    wstate = {"issued": 0, "taken": 0}

    def w_issue_upto(n):
        while wstate["issued"] < min(n, len(seq)):
            i = wstate["issued"]
            k, l, idx = seq[i]
            slot = i % NSLOT
            c = KINDS[k][1]
            S.op("sp", lambda e, k=k, l=l, idx=idx, slot=slot, c=c: e.dma_start(out=wsl[:, slot, 0:c], in_=wb[k][l, idx]),
                 reads=[WB[(k, l)]], writes=[wslb[slot]], dma=True, arena=False)
            wstate["issued"] += 1

    def w_take(kind, l, idx):
        i = wstate["taken"]
        assert seq[i] == (kind, l, idx), (seq[i], kind, l, idx)
        w_issue_upto(i + NSLOT)
        wstate["taken"] += 1
        slot = i % NSLOT
        return wsl[:, slot, :], wslb[slot]

    def mm(out, lhsT, rhs, start, stop, reads, writes):
        S.op("pe", lambda e: e.matmul(out, lhsT, rhs, start=start, stop=stop), reads=reads, writes=writes)

    rr = {"ev": 0}

    def lcol(l, off):
        return l * CW + off

    def mixers(l, g, a, Gw):
        par = l % 2
        npr = Gw // 128
        i0 = a // 128
        deps = [PROJ[par][gg] for gg in (g - 1, g, g + 1) if 0 <= gg < 10]
        with ExitStack() as ms:
            def msb(name, shape, dt):
                return ms.enter_context(nc.sbuf_tensor(f"{name}_{l}_{g}", list(shape), dt))
            naQ = msb("naQ", [128, 4, Gw], BF16)
            naK = msb("naK", [128, 4, Gw + 512], BF16)
            naV = msb("naV", [128, npr + 4, 512], BF16)
            nab = msb("nab", [128, 13, 8, 128], BF16)
            swQ = msb("swQ", [128, 4, Gw], BF16)
            swK = msb("swK", [128, Gw + 256], BF16)
            swV = msb("swV", [128, npr + 2, 128], BF16)
            pin = msb("pin", [128, 4, Gw + 24], BF16)
            cin = msb("cin", [128, 12, Gw + 2], BF16)
            Pna = msb("Pna", [128, 2, 1280], BF16)
            Psw = msb("Psw", [128, 2, 3, 512], BF16)
            rec = msb("rec", [128, 2, 512], F32)
            pw = msb("pw", [128, 3, Gw + 24], F32)
            pooled = msb("pooled", [128, 2, Gw], BF16)
            cw_ = msb("cw", [128, 1, 2, Gw + 2], F32)
            bq, bk, bv, bnab, bsq, bsk, bsv, bpin, bcin = [Buf() for _ in range(9)]
            bP = [Buf(), Buf()]
            bPs = [Buf(), Buf()]
            brec = [Buf(), Buf()]
            bpw = [Buf() for _ in range(3)]
            bpooled = [Buf(), Buf()]
            bcw = [[Buf(), Buf()]]
            b_hi = a + Gw
            klo = max(0, a - 256)
            S.op("sp", lambda e: e.dma_start(out=naQ[:], in_=PT[par][0:4, :, a:b_hi].rearrange("c p n -> p c n")),
                 reads=deps, writes=[bq], dma=True)
            S.op("sp", lambda e: e.dma_start(out=naK[:, :, klo - (a - 256):Gw + 512],
                                             in_=PT[par][4:8, :, klo:b_hi + 256].rearrange("c p n -> p c n")),
                 reads=deps, writes=[bk], dma=True)
            c_lo = max(0, i0 - 2)
            S.op("sp", lambda e: e.dma_start(out=naV[:, c_lo - (i0 - 2):npr + 4, :],
                                             in_=VNA[par][c_lo:i0 + npr + 2].rearrange("c p n -> p c n")),
                 reads=deps, writes=[bv], dma=True)
            ntile = 13 if g == 0 else 5
            S.op("sp", lambda e: e.dma_start(out=nab[:, 0:ntile].rearrange("p t h k -> p (t h k)"),
                                             in_=nab_b[l][:, 0:ntile * 1024]),
                 reads=[WB[("nab", l)]], writes=[bnab], dma=True)
            S.op("sp", lambda e: e.dma_start(out=swQ[:], in_=PT[par][12:16, :, a:b_hi].rearrange("c p n -> p c n")),
                 reads=deps, writes=[bsq], dma=True)
            slo = max(0, a - 128)
            S.op("sp", lambda e: e.dma_start(out=swK[:, slo - (a - 128):Gw + 256], in_=PT[par][16, :, slo:b_hi + 128]),
                 reads=deps, writes=[bsk], dma=True)
            s_lo = max(0, i0 - 1)
            S.op("sp", lambda e: e.dma_start(out=swV[:, s_lo - (i0 - 1):npr + 2, :],
                                             in_=VSW[par][s_lo:i0 + npr + 1].rearrange("c p n -> p c n")),
                 reads=deps, writes=[bsv], dma=True)
            plo = max(0, a - 8)
            if a == 0:
                S.op("pool", lambda e: e.memset(pin[:, :, 0:8], 0.0), writes=[bpin])
                S.op("dve", lambda e: e.memset(cin[:, :, 0:1], 0.0), writes=[bcin])
            S.op("sp", lambda e: e.dma_start(out=pin[:, :, plo - (a - 8):Gw + 24],
                                             in_=PT[par][8:12, :, plo:b_hi + 16].rearrange("c p n -> p c n")),
                 reads=deps, writes=[bpin], dma=True)
            clo = max(0, a - 1)
            S.op("sp", lambda e: e.dma_start(out=cin[:, :, clo - (a - 1):Gw + 2],
                                             in_=PT[par][17:29, :, clo:b_hi + 1].rearrange("c p n -> p c n")),
                 reads=deps, writes=[bcin], dma=True)

            na_its = []
            for i in range(i0, i0 + npr):
                for j in range(4):
                    na_its.append((i, j))

            def na_cfg(i):
                if i == 0:
                    return [0, 1, 2, 3], [5, 6, 7, 8]
                if i == 1:
                    return [0, 1, 2, 3], [9, 10, 11, 12]
                return [i - 2, i - 1, i, i + 1, i + 2], [0, 1, 2, 3, 4]

            def na_S(it):
                i, j = na_its[it]
                clist, tiles = na_cfg(i)
                ncl = len(clist)
                tq = (i - i0) * 128
                r = it % 2
                base = 3 * r
                for hh in range(2):
                    h = 2 * j + hh
                    pb = hh * 64
                    for ci, c in enumerate(clist):
                        t = hh * ncl + ci
                        bank = base + t // 4
                        col = (t % 4) * 128
                        kc = (c - (i0 - 2)) * 128
                        mm(ps[:, bank, col:col + 128], naK[pb:pb + 64, j, kc:kc + 128], naQ[pb:pb + 64, j, tq:tq + 128],
                           True, False, [bk, bq], [psb[bank]])
                        mm(ps[:, bank, col:col + 128], nab[:, tiles[ci], h, :], ident, False, True,
                           [bnab, cbb], [psb[bank]])
                ntl = 2 * ncl
                for bi in range((ntl + 3) // 4):
                    w_ = min(4, ntl - bi * 4) * 128
                    S.op("act", lambda e: e.activation(out=Pna[:, r, bi * 512:bi * 512 + w_], in_=ps[:, base + bi, 0:w_], func=AF.Exp),
                         reads=[psb[base + bi]], writes=[bP[r]])

            def na_rest(it):
                i, j = na_its[it]
                clist, tiles = na_cfg(i)
                ncl = len(clist)
                tq = (i - i0) * 128
                r = it % 2
                odb = 6 + r
                ntl = 2 * ncl
                P4 = Pna[:, r, 0:ntl * 128].rearrange("p (h c q) -> p h c q", h=2, c=ncl)
                for ci in range(ncl):
                    mm(ps[0:64, odb, 256:512].rearrange("p (h q) -> p h q", h=2), ones[:, 0:64], P4[:, :, ci, :],
                       ci == 0, ci == ncl - 1, [bP[r], cbb], [psb[odb]])
                for hh in range(2):
                    h = 2 * j + hh
                    for ci, c in enumerate(clist):
                        mm(ps[0:64, odb, hh * 128:hh * 128 + 128], naV[:, c - (i0 - 2), h * 64:h * 64 + 64],
                           P4[:, hh, ci, :], ci == 0, ci == ncl - 1, [bP[r], bv], [psb[odb]])
                S.op("dve", lambda e: e.reciprocal(out=rec[0:64, r, 0:256], in_=ps[0:64, odb, 256:512]),
                     reads=[psb[odb]], writes=[brec[r]])
                for hh in range(2):
                    S.op("dve", lambda e: e.tensor_tensor(
                        out=yT[hh * 64:hh * 64 + 64, j, tq:tq + 128], in0=ps[0:64, odb, hh * 128:hh * 128 + 128],
                        in1=rec[0:64, r, hh * 128:hh * 128 + 128], op=ALU.mult),
                        reads=[psb[odb], brec[r]], writes=[yTb[j]])

            na_S(0)
            for it in range(len(na_its)):
                if it + 1 < len(na_its):
                    na_S(it + 1)
                na_rest(it)

            sw_its = []
            for n in range(i0, i0 + npr):
                for kv in range(2):
                    sw_its.append((n, kv))

            def sw_S(it):
                n, kv = sw_its[it]
                jl = [-1, 0, 1] if n > 0 else [0, 1]
                tq = (n - i0) * 128
                pb = kv * 64
                base = kv * 3
                for ji, jj in enumerate(jl):
                    kc = (n + jj) * 128 - (a - 128)
                    o3 = ps[:, base + ji, :].rearrange("p (h q) -> p h q", h=4)
                    mm(o3, swK[pb:pb + 64, kc:kc + 128], swQ[pb:pb + 64, :, tq:tq + 128], True, jj == 0,
                       [bsk, bsq], [psb[base + ji]])
                    if jj != 0:
                        mm(o3, maskq[0 if jj < 0 else 1], bcast(ident, 1, 4), False, True, [cbb], [psb[base + ji]])
                    S.op("act", lambda e: e.activation(out=Psw[:, kv, ji, :], in_=ps[:, base + ji, :], func=AF.Exp),
                         reads=[psb[base + ji]], writes=[bPs[kv]])

            def sw_rest(it):
                n, kv = sw_its[it]
                jl = [-1, 0, 1] if n > 0 else [0, 1]
                tq = (n - i0) * 128
                nj = len(jl)
                for ji, jj in enumerate(jl):
                    mm(ps[0:64, 6, :], swV[:, n + jj - (i0 - 1), kv * 64:kv * 64 + 64], Psw[:, kv, ji, :],
                       ji == 0, ji == nj - 1, [bsv, bPs[kv]], [psb[6]])
                for ji, jj in enumerate(jl):
                    mm(ps[0:64, 7, :], ones[:, 0:64], Psw[:, kv, ji, :], ji == 0, False, [cbb, bPs[kv]], [psb[7]])
                mm(ps[0:64, 7, :].rearrange("p (h q) -> p h q", h=4), ones[0:1, 0:64], bcast(sinke[0:1, l * 8 + kv * 4:l * 8 + kv * 4 + 4], 2, 128),
                   False, True, [cbb, sinkb], [psb[7]])
                S.op("dve", lambda e: e.reciprocal(out=rec[0:64, kv, :], in_=ps[0:64, 7, :]),
                     reads=[psb[7]], writes=[brec[kv]])
                for hh in range(4):
                    h = kv * 4 + hh
                    ch = 8 + h // 2
                    po = (h % 2) * 64
                    S.op("dve", lambda e: e.tensor_tensor(
                        out=yT[po:po + 64, ch, tq:tq + 128], in0=ps[0:64, 6, hh * 128:hh * 128 + 128],
                        in1=rec[0:64, kv, hh * 128:hh * 128 + 128], op=ALU.mult),
                        reads=[psb[6], brec[kv]], writes=[yTb[ch]])

            sw_S(0)
            for it in range(len(sw_its)):
                if it + 1 < len(sw_its):
                    sw_S(it + 1)
                sw_rest(it)

            W_ = Gw + 24
            cgo = L * CW
            for g4 in range(4):
                u = pin[:, g4, :]
                A_, B_, C_ = pw[:, 0, :], pw[:, 1, :], pw[:, 2, :]
                S.op("pool", lambda e, u=u: e.tensor_tensor(out=A_[:, 1:W_], in0=u[:, 0:W_ - 1], in1=u[:, 1:W_], op=ALU.add),
                     reads=[bpin], writes=[bpw[0]])
                cur, curb, oth, othb = A_, bpw[0], B_, bpw[1]
                if g4 >= 1:
                    S.op("pool", lambda e, cur=cur, oth=oth: e.tensor_tensor(
                        out=oth[:, 2:W_ - 1], in0=cur[:, 1:W_ - 2], in1=cur[:, 3:W_], op=ALU.add),
                        reads=[curb], writes=[othb])
                    cur, curb, oth, othb = oth, othb, cur, curb
                if g4 >= 2:
                    S.op("pool", lambda e, cur=cur, oth=oth: e.tensor_tensor(
                        out=oth[:, 4:W_ - 3], in0=cur[:, 2:W_ - 5], in1=cur[:, 6:W_ - 1], op=ALU.add),
                        reads=[curb], writes=[othb])
                    cur, curb, oth, othb = oth, othb, cur, curb
                if g4 >= 3:
                    S.op("pool", lambda e, cur=cur, oth=oth: e.tensor_tensor(
                        out=oth[:, 8:W_ - 7], in0=cur[:, 4:W_ - 11], in1=cur[:, 12:W_ - 3], op=ALU.add),
                        reads=[curb], writes=[othb])
                    cur, curb, oth, othb = oth, othb, cur, curb
                S.op("pool", lambda e, cur=cur, g4=g4: e.tensor_scalar(
                    out=C_[:, 0:Gw], in0=cur[:, 8:8 + Gw], scalar1=cf[:, cgo + g4:cgo + g4 + 1], scalar2=None, op0=ALU.mult),
                    reads=[curb, cfb], writes=[bpw[2]])
                S.op("pool", lambda e, cur=cur, oth=oth, g4=g4: e.tensor_scalar(
                    out=oth[:, 0:Gw], in0=cur[:, 9:9 + Gw], scalar1=cf[:, cgo + 4 + g4:cgo + 5 + g4], scalar2=None, op0=ALU.mult),
                    reads=[curb, cfb], writes=[othb])
                S.op("pool", lambda e, oth=oth: e.tensor_tensor(out=C_[:, 0:Gw], in0=C_[:, 0:Gw], in1=oth[:, 0:Gw], op=ALU.add),
                     reads=[othb, bpw[2]], writes=[bpw[2]])
                if a == 0:
                    ta = cf[:, cgo + 8 + g4 * 16:cgo + 8 + g4 * 16 + 16]
                    tb = cf[:, cgo + 72 + g4 * 16:cgo + 72 + g4 * 16 + 16]
                    S.op("pool", lambda e, cur=cur, ta=ta: e.tensor_tensor(out=C_[:, 0:16], in0=cur[:, 8:24], in1=ta, op=ALU.mult),
                         reads=[curb, cfb, bpw[2]], writes=[bpw[2]])
                    S.op("pool", lambda e, cur=cur, tb=tb, oth=oth: e.tensor_tensor(
                        out=oth[:, 0:16], in0=cur[:, 9:25], in1=tb, op=ALU.mult), reads=[curb, cfb], writes=[othb])
                    S.op("pool", lambda e, oth=oth: e.tensor_tensor(out=C_[:, 0:16], in0=C_[:, 0:16], in1=oth[:, 0:16], op=ALU.add),
                         reads=[othb, bpw[2]], writes=[bpw[2]])
                pr = g4 % 2
                S.op("pool", lambda e, u=u, pr=pr: e.tensor_tensor(out=pooled[:, pr, :], in0=C_[:, 0:Gw], in1=u[:, 8:8 + Gw],
                                                                 op=ALU.subtract),
                     reads=[bpw[2], bpin], writes=[bpooled[pr]])
                bank = 4 + pr
                mm(ps[:, bank, 0:Gw], poolw[:, l * 512 + g4 * 128:l * 512 + g4 * 128 + 128], pooled[:, pr, :], True, True,
                   [poolwb, bpooled[pr]], [psb[bank]])
                S.op("act", lambda e, g4=g4, bank=bank: e.activation(
                    out=yT[:, 4 + g4, 0:Gw], in_=ps[:, bank, 0:Gw], func=AF.Copy,
                    scale=cf[:, lcol(l, 160) + g4:lcol(l, 160) + g4 + 1]),
                    reads=[psb[bank], cfb], writes=[yTb[4 + g4]])

            for cc in range(4):
                r = 0
                z, acc = cw_[:, r, 0, :], cw_[:, r, 1, :]
                hch, bch, cch = cin[:, cc, :], cin[:, 4 + cc, :], cin[:, 8 + cc, :]
                wc = lambda tap, cc=cc: cf[:, lcol(l, 164) + tap * 4 + cc:lcol(l, 164) + tap * 4 + cc + 1]
                S.op("dve", lambda e, z=z, hch=hch, cch=cch: e.tensor_tensor(out=z, in0=cch, in1=hch, op=ALU.mult),
                     reads=[bcin], writes=[bcw[r][0]])
                S.op("dve", lambda e, z=z, acc=acc, wc=wc: e.tensor_scalar(
                    out=acc[:, 0:Gw], in0=z[:, 1:Gw + 1], scalar1=wc(1), scalar2=None, op0=ALU.mult),
                    reads=[bcw[r][0], cfb], writes=[bcw[r][1]])
                S.op("dve", lambda e, z=z, acc=acc, wc=wc: e.scalar_tensor_tensor(
                    out=acc[:, 0:Gw], in0=z[:, 0:Gw], scalar=wc(0), in1=acc[:, 0:Gw], op0=ALU.mult, op1=ALU.add),
                    reads=[bcw[r][0], cfb, bcw[r][1]], writes=[bcw[r][1]])
                S.op("dve", lambda e, z=z, acc=acc, wc=wc: e.scalar_tensor_tensor(
                    out=acc[:, 0:Gw], in0=z[:, 2:Gw + 2], scalar=wc(2), in1=acc[:, 0:Gw], op0=ALU.mult, op1=ALU.add),
                    reads=[bcw[r][0], cfb, bcw[r][1]], writes=[bcw[r][1]])
                S.op("dve", lambda e, acc=acc, bch=bch, cc=cc: e.tensor_tensor(
                    out=yT[:, 12 + cc, 0:Gw], in0=acc[:, 0:Gw], in1=bch[:, 1:Gw + 1], op=ALU.mult),
                    reads=[bcw[r][1], bcin], writes=[yTb[12 + cc]])
            if debug and l == 0:
                S.op("sp", lambda e: e.dma_start(out=YD[:, :, a:a + Gw].rearrange("c p n -> p c n"), in_=yT[:, :, 0:Gw]),
                     reads=yTb, writes=[Buf()], dma=True)
            S.op("dve", lambda e: e.memset(rec[0:1, 0, 0:1], 0.0), writes=[ARENA, brec[0]], arena=False)

    def layer_norm(l, which, Gw, tk):
        sq, sqb, mst, mstb = tk["sq"], tk["sqb"], tk["mst"], tk["mstb"]
        gcol = lcol(l, 64 + which * 32)
        bcol = gcol + 16
        for q in range(4):
            cs = slice(q * 4, q * 4 + 4)
            S.op("pool", lambda e, cs=cs: e.tensor_tensor(out=sq[:, cs, 0:Gw], in0=xT[:, cs, 0:Gw], in1=xT[:, cs, 0:Gw], op=ALU.mult),
                 reads=xTb[cs], writes=sqb[cs])
            S.op("act", lambda e, cs=cs: e.activation(out=uT[:, cs, 0:Gw], in_=xT[:, cs, 0:Gw], func=AF.Copy),
                 reads=xTb[cs], writes=uTb[cs])
        for k in range(NCH):
            mm(ps[:, 6, 0:Gw], onesdiv, uT[:, k, 0:Gw], k == 0, k == NCH - 1, [cbb, uTb[k]], [psb[6]])
        for k in range(NCH):
            mm(ps[:, 7, 0:Gw], onesdiv, sq[:, k, 0:Gw], k == 0, k == NCH - 1, [cbb, sqb[k]], [psb[7]])
        m, var, rstd = mst[:, 0, 0:Gw], mst[:, 1, 0:Gw], mst[:, 2, 0:Gw]
        S.op("dve", lambda e: e.tensor_copy(out=m, in_=ps[:, 6, 0:Gw]), reads=[psb[6]], writes=[mstb[0]])
        S.op("dve", lambda e: e.tensor_tensor(out=var, in0=m, in1=m, op=ALU.mult), reads=[mstb[0]], writes=[mstb[1]])
        S.op("dve", lambda e: e.tensor_tensor(out=var, in0=ps[:, 7, 0:Gw], in1=var, op=ALU.subtract),
             reads=[psb[7], mstb[1]], writes=[mstb[1]])
        S.op("dve", lambda e: e.tensor_scalar(out=var, in0=var, scalar1=EPS, scalar2=None, op0=ALU.add),
             reads=[mstb[1]], writes=[mstb[1]])
        S.op("act", lambda e: e.activation(out=var, in_=var, func=AF.Sqrt), reads=[mstb[1]], writes=[mstb[1]])
        S.op("dve", lambda e: e.reciprocal(out=rstd, in_=var), reads=[mstb[1]], writes=[mstb[2]])
        for q in range(4):
            cs = slice(q * 4, q * 4 + 4)
            S.op("dve", lambda e, cs=cs: e.tensor_tensor(out=xT[:, cs, 0:Gw], in0=xT[:, cs, 0:Gw], in1=bcast(m, 1, 4), op=ALU.subtract),
                 reads=[mstb[0]], writes=xTb[cs])
            S.op("pool", lambda e, cs=cs: e.tensor_tensor(out=xT[:, cs, 0:Gw], in0=xT[:, cs, 0:Gw], in1=bcast(rstd, 1, 4), op=ALU.mult),
                 reads=[mstb[2]], writes=xTb[cs])
            for dc in range(q * 4, q * 4 + 4):
                S.op("act", lambda e, dc=dc: e.activation(out=uT[:, dc, 0:Gw], in_=xT[:, dc, 0:Gw], func=AF.Identity,
                                                          scale=cf[:, gcol + dc:gcol + dc + 1], bias=cf[:, bcol + dc:bcol + dc + 1]),
                     reads=[xTb[dc], cfb], writes=[uTb[dc]])
        for q in range(4):
            cs = slice(q * 4, q * 4 + 4)
            S.op("dve", lambda e, cs=cs, q=q: e.tensor_tensor(out=xT[:, cs, 0:Gw], in0=xT[:, cs, 0:Gw],
                                                              in1=bcast(cf[:, gcol + q * 4:gcol + q * 4 + 4], 2, Gw), op=ALU.mult),
                 reads=[cfb] + uTb[cs], writes=xTb[cs])
            S.op("pool", lambda e, cs=cs, q=q: e.tensor_tensor(out=xT[:, cs, 0:Gw], in0=xT[:, cs, 0:Gw],
                                                               in1=bcast(cf[:, bcol + q * 4:bcol + q * 4 + 4], 2, Gw), op=ALU.add),
                 reads=[cfb], writes=xTb[cs])

    def in_proj(lp, g, a, Gw, tk):
        par = lp % 2
        stg, stgb, rt, rtb, cosb_, sinb_, tabb = tk["stg"], tk["stgb"], tk["rt"], tk["rtb"], tk["cos"], tk["sin"], tk["tabb"]
        vst, vstb, vs2, vs2b = tk["vst"], tk["vstb"], tk["vs2"], tk["vs2b"]
        ntt = Gw // 128
        S.op("sp", lambda e: e.dma_start(out=cosb_[:, 0:Gw], in_=cos_in[:, a:a + Gw]), writes=[tabb], dma=True)
        S.op("sp", lambda e: e.dma_start(out=sinb_[:, 0:Gw], in_=sin_in[:, a:a + Gw]), writes=[tabb], dma=True)
        pbuf = PROJ[par][g]
        for cc in range(29):
            slab, slb = w_take("win", lp, cc)
            if cc >= cfg.get("ip_n", 99):
                continue
            if tk.get("p0tick"):
                p0_tick()
            bank = cc % 4
            for k in range(NCH):
                mm(ps[:, bank, 0:Gw], slab[:, k * 128:k * 128 + 128], uT[:, k, 0:Gw], k == 0, k == NCH - 1,
                   [slb, uTb[k]], [psb[bank]])
            so = cc % 4
            sc = 0.125 if (cc < 4 or 12 <= cc < 16) else 1.0
            if 12 <= cc <= 16:
                S.op("act", lambda e, bank=bank, sc=sc: e.activation(out=rt[:, 0, 0:Gw], in_=ps[:, bank, 0:Gw], func=AF.Copy, scale=sc),
                     reads=[psb[bank]], writes=[rtb[0]])
                mm(ps[:, 5, 0:Gw], Rt, rt[:, 0, 0:Gw], True, True, [cbb, rtb[0]], [psb[5]])
                t1, t2 = tk["t1"], tk["t2"]
                S.op("dve", lambda e: e.tensor_tensor(out=t1[:, 0:Gw], in0=rt[:, 0, 0:Gw], in1=cosb_[:, 0:Gw], op=ALU.mult),
                     reads=[rtb[0], tabb], writes=[tk["t1b"]])
                S.op("dve", lambda e: e.tensor_tensor(out=t2[:, 0:Gw], in0=ps[:, 5, 0:Gw], in1=sinb_[:, 0:Gw], op=ALU.mult),
                     reads=[psb[5], tabb], writes=[tk["t2b"]])
                S.op("dve", lambda e, so=so: e.tensor_tensor(out=stg[:, so, 0:Gw], in0=t1[:, 0:Gw], in1=t2[:, 0:Gw], op=ALU.add),
                     reads=[tk["t1b"], tk["t2b"]], writes=[stgb[so]])
            else:
                rr["ev"] += 1
                if rr["ev"] % 2 == 0:
                    S.op("act", lambda e, bank=bank, so=so, sc=sc: e.activation(
                        out=stg[:, so, 0:Gw], in_=ps[:, bank, 0:Gw], func=AF.Copy, scale=sc),
                        reads=[psb[bank]], writes=[stgb[so]])
                else:
                    S.op("dve", lambda e, bank=bank, so=so, sc=sc: e.tensor_scalar(
                        out=stg[:, so, 0:Gw], in0=ps[:, bank, 0:Gw], scalar1=sc, scalar2=None, op0=ALU.mult),
                        reads=[psb[bank]], writes=[stgb[so]])
            S.op("sp", lambda e, cc=cc, so=so: e.dma_start(out=PT[par][cc, :, a:a + Gw], in_=stg[:, so, 0:Gw]),
                 reads=[stgb[so]], writes=[pbuf], dma=True)
        if not cfg.get("ip_v", True):
            for s in range(5):
                w_take("win", lp, 29 + s)
            return
        for s in range(4):
            slab, slb = w_take("win", lp, 29 + s)
            for tt in range(ntt):
                for kk in range(4):
                    k = s * 4 + kk
                    mm(ps[:, 4 + tt, :], uT[:, k, tt * 128:tt * 128 + 128], slab[:, kk * 512:kk * 512 + 512],
                       k == 0, k == NCH - 1, [slb, uTb[k]], [psb[4 + tt]])
        for tt in range(ntt):
            if tt % 2 == 0:
                S.op("act", lambda e, tt=tt: e.activation(out=vst[:, tt, :], in_=ps[:, 4 + tt, :], func=AF.Copy),
                     reads=[psb[4 + tt]], writes=[vstb])
            else:
                S.op("dve", lambda e, tt=tt: e.tensor_copy(out=vst[:, tt, :], in_=ps[:, 4 + tt, :]),
                     reads=[psb[4 + tt]], writes=[vstb])
        S.op("sp", lambda e: e.dma_start(out=VNA[par][a // 128:a // 128 + ntt].rearrange("c p n -> p c n"), in_=vst[:, 0:ntt, :]),
             reads=[vstb], writes=[pbuf], dma=True)
        slab, slb = w_take("win", lp, 33)
        for tt in range(ntt):
            for k in range(NCH):
                mm(ps[:, 0, tt * 128:tt * 128 + 128], uT[:, k, tt * 128:tt * 128 + 128], slab[:, k * 128:k * 128 + 128],
                   k == 0, k == NCH - 1, [slb, uTb[k]], [psb[0]])
        S.op("act", lambda e: e.activation(out=vs2[:, 0:ntt, :], in_=ps[:, 0, 0:ntt * 128].rearrange("p (t n) -> p t n", n=128), func=AF.Copy),
             reads=[psb[0]], writes=[vs2b])
        S.op("sp", lambda e: e.dma_start(out=VSW[par][a // 128:a // 128 + ntt].rearrange("c p n -> p c n"), in_=vs2[:, 0:ntt, :]),
             reads=[vs2b], writes=[pbuf], dma=True)

    def alloc_tok(ms, tag, full):
        def msb(name, shape, dt):
            return ms.enter_context(nc.sbuf_tensor(f"{name}_{tag}", list(shape), dt))
        tk = {}
        tk["stg"] = msb("stg", [128, 4, 512], BF16); tk["stgb"] = [Buf() for _ in range(4)]
        tk["rt"] = msb("rt", [128, 1, 512], BF16); tk["rtb"] = [Buf()]
        tk["cos"] = msb("cos", [128, 512], F32); tk["sin"] = msb("sin", [128, 512], F32); tk["tabb"] = Buf()
        tk["t1"] = msb("t1", [128, 512], F32); tk["t2"] = msb("t2", [128, 512], F32); tk["t1b"] = Buf(); tk["t2b"] = Buf()
        tk["vst"] = msb("vst", [128, 4, 512], BF16); tk["vstb"] = Buf()
        tk["vs2"] = msb("vs2", [128, 4, 128], BF16); tk["vs2b"] = Buf()
        nb_ = 2 if full else 6
        p0_bind(msb("p0f", [128, nb_, 1280], F32), msb("p0b", [128, nb_, 1280], BF16), nb_)
        p0s["use_pool"] = not full
        if full:
            tk["mixT"] = msb("mixT", [128, NCH, 512], BF16); tk["mixb"] = [Buf() for _ in range(NCH)]
            tk["sq"] = msb("hT", [128, NCH, 512], BF16); tk["sqb"] = [Buf() for _ in range(NCH)]
            tk["mst"] = msb("mst", [128, 3, 512], F32); tk["mstb"] = [Buf() for _ in range(3)]
            tk["gt"] = msb("gt", [128, 2, 512], F32); tk["gtb"] = [Buf(), Buf()]
            tk["pd"] = msb("pd", [128, 2, 512], F32); tk["pdb"] = [Buf(), Buf()]
            tk["ac"] = msb("ac", [128, 2, 512], F32); tk["acb"] = [Buf(), Buf()]
            tk["qT"] = msb("qT", [128, 4, 512], BF16); tk["qTb"] = [Buf() for _ in range(4)]
            tk["oT"] = msb("oT", [128, 4, 512], BF16); tk["oTb"] = [Buf() for _ in range(4)]
            tk["Pc"] = msb("Pc", [128, 2, 2, 512], BF16); tk["Pcb"] = [Buf(), Buf()]
            tk["rc"] = msb("rc", [128, 512], F32); tk["rcb"] = Buf()
            tk["rl"] = msb("rl", [128, 2, 512], F32); tk["rlb"] = [Buf(), Buf()]
        return tk

    def t1_load(l, g, a, Gw):
        src = xin if l == 0 else XR
        S.op("sp", lambda e: e.dma_start(out=xT[:, :, 0:Gw], in_=src[:, :, a:a + Gw].rearrange("c p n -> p c n")),
             reads=([] if l == 0 else [XRb[g]]), writes=xTb, dma=True)
        for q in range(4):
            cs = slice(q * 4, q * 4 + 4)
            S.op("act", lambda e, cs=cs: e.activation(out=uT[:, cs, 0:Gw], in_=xT[:, cs, 0:Gw], func=AF.Copy),
                 reads=xTb[cs], writes=uTb[cs])

    def tok_stage(l, g, a, Gw):
        with ExitStack() as ms:
            tk = alloc_tok(ms, f"{l}_{g}", True)
            mixT, mixb, hT, hTb = tk["mixT"], tk["mixb"], tk["sq"], tk["sqb"]
            gt, gtb, pd, pdb, ac, acb = tk["gt"], tk["gtb"], tk["pd"], tk["pdb"], tk["ac"], tk["acb"]
            n2 = 0
            for dc in range(NCH):
                for i in range(4):
                    slab, slb = w_take("wgb", l, dc * 4 + i)
                    p0_tick()
                    r = n2 % 2
                    n2 += 1
                    gb_, bb_ = r, 2 + r
                    for k in range(NCH):
                        mm(ps[:, gb_, 0:Gw], slab[:, k * 128:k * 128 + 128], uT[:, k, 0:Gw], k == 0, k == NCH - 1,
                           [slb, uTb[k]], [psb[gb_]])
                    for k in range(4):
                        mm(ps[:, bb_, 0:Gw], slab[:, 2048 + k * 128:2048 + k * 128 + 128], yT[:, i * 4 + k, 0:Gw], k == 0, k == 3,
                           [slb, yTb[i * 4 + k]], [psb[bb_]])
                    bgc = lcol(l, i * 16 + dc)
                    S.op("act", lambda e, r=r, gb_=gb_, bgc=bgc: e.activation(
                        out=gt[:, r, 0:Gw], in_=ps[:, gb_, 0:Gw], func=AF.Sigmoid, bias=cf[:, bgc:bgc + 1]),
                        reads=[psb[gb_], cfb], writes=[gtb[r]])
                    ar = dc % 2
                    if i == 0:
                        S.op("dve", lambda e, r=r, bb_=bb_, ar=ar: e.tensor_tensor(
                            out=ac[:, ar, 0:Gw], in0=ps[:, bb_, 0:Gw], in1=gt[:, r, 0:Gw], op=ALU.mult),
                            reads=[psb[bb_], gtb[r]], writes=[acb[ar]])
                    else:
                        S.op("dve", lambda e, r=r, bb_=bb_: e.tensor_tensor(
                            out=pd[:, r, 0:Gw], in0=ps[:, bb_, 0:Gw], in1=gt[:, r, 0:Gw], op=ALU.mult),
                            reads=[psb[bb_], gtb[r]], writes=[pdb[r]])
                        if i < 3:
                            S.op("pool", lambda e, r=r, ar=ar: e.tensor_tensor(
                                out=ac[:, ar, 0:Gw], in0=ac[:, ar, 0:Gw], in1=pd[:, r, 0:Gw], op=ALU.add),
                                reads=[pdb[r], acb[ar]], writes=[acb[ar]])
                        else:
                            S.op("pool", lambda e, r=r, ar=ar, dc=dc: e.tensor_tensor(
                                out=mixT[:, dc, 0:Gw], in0=ac[:, ar, 0:Gw], in1=pd[:, r, 0:Gw], op=ALU.add),
                                reads=[pdb[r], acb[ar]], writes=[mixb[dc]])
            for dc in range(NCH):
                slab, slb = w_take("wout", l, dc)
                bank = 4 + dc % 2
                for k in range(NCH):
                    mm(ps[:, bank, 0:Gw], slab[:, k * 128:k * 128 + 128], mixT[:, k, 0:Gw], k == 0, k == NCH - 1,
                       [slb, mixb[k]], [psb[bank]])
                S.op("dve", lambda e, dc=dc, bank=bank: e.scalar_tensor_tensor(
                    out=xT[:, dc, 0:Gw], in0=xT[:, dc, 0:Gw], scalar=ALPHA, in1=ps[:, bank, 0:Gw], op0=ALU.mult, op1=ALU.add),
                    reads=[psb[bank]], writes=[xTb[dc]])
            layer_norm(l, 0, Gw, tk)
            qT, qTb, oT, oTb, Pc, Pcb, rc, rcb = tk["qT"], tk["qTb"], tk["oT"], tk["oTb"], tk["Pc"], tk["Pcb"], tk["rc"], tk["rcb"]
            for h in range(4):
                slab, slb = w_take("wq", l, h)
                bank = h % 2
                for k in range(NCH):
                    mm(ps[:, bank, 0:Gw], slab[:, k * 128:k * 128 + 128], uT[:, k, 0:Gw], k == 0, k == NCH - 1,
                       [slb, uTb[k]], [psb[bank]])
                S.op("act", lambda e, h=h, bank=bank: e.activation(out=qT[:, h, 0:Gw], in_=ps[:, bank, 0:Gw], func=AF.Copy),
                     reads=[psb[bank]], writes=[qTb[h]])
            def ca_S(h):
                r = h % 2
                for mc in range(2):
                    bank = 2 + r * 2 + mc
                    mm(ps[:, bank, 0:Gw], KT[:, h, mc * 128:mc * 128 + 128], qT[:, h, 0:Gw], True, True, [KTb, qTb[h]], [psb[bank]])
                    S.op("act", lambda e: e.activation(out=Pc[:, r, mc, 0:Gw], in_=ps[:, bank, 0:Gw], func=AF.Exp, scale=float(128 ** -0.5)),
                         reads=[psb[bank]], writes=[Pcb[r]])

            def ca_rest(h):
                r = h % 2
                for mc in range(2):
                    mm(ps[:, 6, 0:Gw], Vm[:, mc, h * 128:h * 128 + 128], Pc[:, r, mc, 0:Gw], mc == 0, mc == 1, [Vmb, Pcb[r]], [psb[6]])
                for mc in range(2):
                    mm(ps[:, 7, 0:Gw], ones, Pc[:, r, mc, 0:Gw], mc == 0, mc == 1, [cbb, Pcb[r]], [psb[7]])
                S.op("dve", lambda e: e.reciprocal(out=rc[:, 0:Gw], in_=ps[:, 7, 0:Gw]), reads=[psb[7]], writes=[rcb])
                S.op("dve", lambda e: e.tensor_tensor(out=oT[:, h, 0:Gw], in0=ps[:, 6, 0:Gw], in1=rc[:, 0:Gw], op=ALU.mult),
                     reads=[psb[6], rcb], writes=[oTb[h]])

            ca_S(0)
            for h in range(4):
                if h + 1 < 4:
                    ca_S(h + 1)
                ca_rest(h)
            for dg in range(4):
                slab, slb = w_take("wo", l, dg)
                for dcc in range(4):
                    dc = dg * 4 + dcc
                    bank = dc % 2
                    for h in range(4):
                        mm(ps[:, bank, 0:Gw], slab[:, (dcc * 4 + h) * 128:(dcc * 4 + h) * 128 + 128], oT[:, h, 0:Gw], h == 0, h == 3,
                           [slb, oTb[h]], [psb[bank]])
                    S.op("dve", lambda e, dc=dc, bank=bank: e.scalar_tensor_tensor(
                        out=xT[:, dc, 0:Gw], in0=xT[:, dc, 0:Gw], scalar=ALPHA, in1=ps[:, bank, 0:Gw], op0=ALU.mult, op1=ALU.add),
                        reads=[psb[bank]], writes=[xTb[dc]])
            layer_norm(l, 1, Gw, tk)
            rl, rlb = tk["rl"], tk["rlb"]
            n5 = 0
            for qq in range(4):
                for fcc in range(16):
                    slab, slb = w_take("w1", l, qq * 16 + fcc)
                    bank = n5 % 4
                    r = n5 % 2
                    n5 += 1
                    for k in range(NCH):
                        mm(ps[:, bank, 0:Gw], slab[:, k * 128:k * 128 + 128], uT[:, k, 0:Gw], k == 0, k == NCH - 1,
                           [slb, uTb[k]], [psb[bank]])
                    S.op("act", lambda e, r=r, bank=bank: e.activation(out=rl[:, r, 0:Gw], in_=ps[:, bank, 0:Gw], func=AF.Relu),
                         reads=[psb[bank]], writes=[rlb[r]])
                    S.op("pool", lambda e, r=r, fcc=fcc: e.tensor_tensor(out=hT[:, fcc, 0:Gw], in0=rl[:, r, 0:Gw], in1=rl[:, r, 0:Gw], op=ALU.mult),
                         reads=[rlb[r]], writes=[hTb[fcc]])
                for dc in range(NCH):
                    slab, slb = w_take("w2", l, qq * 16 + dc)
                    bank = 4 + dc % 4
                    for k in range(16):
                        mm(ps[:, bank, 0:Gw], slab[:, k * 128:k * 128 + 128], hT[:, k, 0:Gw], k == 0, k == 15,
                           [slb, hTb[k]], [psb[bank]])
                    if qq == 0:
                        S.op("dve", lambda e, dc=dc, bank=bank: e.scalar_tensor_tensor(
                            out=xT[:, dc, 0:Gw], in0=xT[:, dc, 0:Gw], scalar=ALPHA, in1=ps[:, bank, 0:Gw], op0=ALU.mult, op1=ALU.add),
                            reads=[psb[bank]], writes=[xTb[dc]])
                    else:
                        S.op("dve", lambda e, dc=dc, bank=bank: e.tensor_tensor(
                            out=xT[:, dc, 0:Gw], in0=xT[:, dc, 0:Gw], in1=ps[:, bank, 0:Gw], op=ALU.add),
                            reads=[psb[bank]], writes=[xTb[dc]])
            layer_norm(l, 2, Gw, tk)
            if l + 1 < L:
                S.op("sp", lambda e: e.dma_start(out=XR[:, :, a:a + Gw].rearrange("c p n -> p c n"), in_=xT[:, :, 0:Gw]),
                     reads=xTb, writes=[XRb[g]], dma=True)
                in_proj(l + 1, g, a, Gw, tk)
            else:
                hi = min(a + Gw, 4096)
                if hi > a:
                    S.op("sp", lambda e: e.dma_start(out=outT[:, :, a:hi].rearrange("c p n -> p c n"), in_=xT[:, :, 0:hi - a]),
                         reads=xTb, writes=[OUTb], dma=True)
            p0_tick(flush=True)
            S.op("dve", lambda e: e.memset(tk["rc"][0:1, 0:1], 0.0), writes=[ARENA, tk["rcb"]], arena=False)

    S.arena = ARENA
    p0_phase(mark_after_layer(0), 29 * min(len(groups_of(TLOC)), cfg.get("npre", 99)))
    for g, (a, Gw) in enumerate(groups_of(TLOC)):
        if g >= cfg.get("npre", 99):
            for _ in range(34):
                wstate["taken"] += 1
                wstate["issued"] = max(wstate["issued"], wstate["taken"])
            continue
        with ExitStack() as ms:
            tk = alloc_tok(ms, f"pre_{g}", False)
            S.op("sp", lambda e: e.dma_start(out=xT[:, :, 0:Gw], in_=xin[:, :, a:a + Gw].rearrange("c p n -> p c n")),
                 writes=xTb, dma=True)
            for q in range(4):
                cs = slice(q * 4, q * 4 + 4)
                S.op("act" if q % 2 == 0 else "dve",
                     (lambda e, cs=cs: e.activation(out=uT[:, cs, 0:Gw], in_=xT[:, cs, 0:Gw], func=AF.Copy)) if q % 2 == 0 else
                     (lambda e, cs=cs: e.tensor_copy(out=uT[:, cs, 0:Gw], in_=xT[:, cs, 0:Gw])),
                     reads=xTb[cs], writes=uTb[cs])
            tk["p0tick"] = True
            in_proj(0, g, a, Gw, tk)
            p0_tick(flush=True)
            S.op("dve", lambda e: e.memset(tk["t1"][0:1, 0:1], 0.0), writes=[ARENA, tk["t1b"]], arena=False)
    for l in range(L):
      with nc.sbuf_tensor(f"memb{l}", [128, NCH, 256], BF16) as memb:
       if not cfg.get("kv", True):
        for _ in range(8):
            wstate["taken"] += 1
            wstate["issued"] = max(wstate["issued"], wstate["taken"])
       else:
        S.op("sp", lambda e: e.dma_start(out=memb[:], in_=memb_b), reads=[membdb], writes=[membb], dma=True)
        for h in range(4):
            slab, slb = w_take("wkvk", l, h)
            bank = h % 2
            for k in range(NCH):
                mm(ps[:, bank, 0:256], slab[:, k * 128:k * 128 + 128], memb[:, k, :], k == 0, k == NCH - 1, [slb, membb], [psb[bank]])
            S.op("act", lambda e, h=h, bank=bank: e.activation(out=KT[:, h, :], in_=ps[:, bank, 0:256], func=AF.Copy),
                 reads=[psb[bank]], writes=[KTb])
        for s in range(4):
            slab, slb = w_take("wkvv", l, s)
            for mc in range(2):
                for kk in range(4):
                    k = s * 4 + kk
                    mm(ps[:, 4 + mc, :], memb[:, k, mc * 128:mc * 128 + 128], slab[:, kk * 512:kk * 512 + 512], k == 0, k == NCH - 1,
                       [slb, membb], [psb[4 + mc]])
        for mc in range(2):
            S.op("act", lambda e, mc=mc: e.activation(out=Vm[:, mc, :], in_=ps[:, 4 + mc, :], func=AF.Copy),
                 reads=[psb[4 + mc]], writes=[Vmb])
        S.op("dve", lambda e: e.memset(memb[0:1, 0, 0:1], 0.0), writes=[ARENA, membb], arena=False)
      p0_phase(mark_after_layer(l + 1), 64 * min(len(groups_of(VL[l])), cfg.get("ngrp", 99)))
      for g, (a, Gw) in enumerate(groups_of(VL[l])):
            if g >= cfg.get("ngrp", 99):
                continue
            if cfg.get("tok", True):
                t1_load(l, g, a, Gw)
            if cfg.get("mix", True):
                mixers(l, g, a, Gw)
            if cfg.get("tok", True):
                tok_stage(l, g, a, Gw)
    if not cfg:
        assert wstate["taken"] == len(seq), (wstate, len(seq))
    S.finish()
    st.close()
    return nc, S


def slab_k(W):
    K, N = W.shape
    return np.ascontiguousarray(W.reshape(K // 128, 128, N // 128, 128).transpose(2, 1, 0, 3)).reshape(N // 128, 128, K)


def prep_weights(inp, L):
    out = {k: [] for k in KINDS}
    for l in range(L):
        g = lambda n: np.asarray(inp[n][l], dtype=np.float32)
        wg, wbr = g("w_gate"), g("w_branch")
        sg = [slab_k(wg[i]) for i in range(4)]
        sbr = [slab_k(wbr[i]) for i in range(4)]
        wgb = np.empty((64, 128, 2560), np.float32)
        for dc in range(16):
            for i in range(4):
                wgb[dc * 4 + i, :, 0:2048] = sg[i][dc]
                wgb[dc * 4 + i, :, 2048:2560] = sbr[i][dc]
        out["wgb"].append(wgb)
        out["wout"].append(slab_k(g("w_mix_out")))
        out["wq"].append(slab_k(g("wq_mem")))
        wkv = g("wkv_mem")
        out["wkvk"].append(slab_k(wkv[:, 0:512]))
        out["wkvv"].append(np.ascontiguousarray(wkv[:, 512:1024].reshape(4, 4, 128, 512).transpose(0, 2, 1, 3)).reshape(4, 128, 2048))
        wo = g("wo_mem")
        out["wo"].append(np.ascontiguousarray(wo.reshape(4, 128, 4, 4, 128).transpose(2, 1, 3, 0, 4)).reshape(4, 128, 2048))
        out["w1"].append(slab_k(g("w_ff1")))
        w2 = g("w_ff2")
        out["w2"].append(np.ascontiguousarray(w2.reshape(4, 16, 128, 16, 128).transpose(0, 3, 2, 1, 4)).reshape(64, 128, 2048))
        win = g("w_in")
        cols = list(range(0, 1024)) + list(range(1536, 2048))
        for c in range(4):
            cols += list(range(2048 + c * 64, 2048 + c * 64 + 64)) + list(range(2048 + (c + 4) * 64, 2048 + (c + 4) * 64 + 64))
        cols += list(range(2560, 2688)) + list(range(2816, 4352))
        fm = slab_k(win[:, np.array(cols)])
        vna = np.ascontiguousarray(win[:, 1024:1536].reshape(4, 4, 128, 512).transpose(0, 2, 1, 3)).reshape(4, 128, 2048)
        vsw = np.ascontiguousarray(win[:, 2688:2816].reshape(16, 128, 128).transpose(1, 0, 2)).reshape(1, 128, 2048)
        out["win"].append(np.concatenate([fm, vna, vsw], axis=0))
    return {k + "_f": np.stack(v) for k, v in out.items()}


def na_tile(rpb, typ, i, c):
    idx = np.arange(128)
    qr, qc = idx // 64, idx % 64
    Rl, KRl = 2 * i + qr, 2 * c + qr
    if typ == 0:
        R, KR, QC, KC = Rl, KRl, qc, qc
    else:
        R, KR, QC, KC = 127 - Rl, 127 - KRl, 63 - qc, 63 - qc
    rs = np.clip(R - 4, 0, 120)
    row_ok = (KR[None, :] >= rs[:, None]) & (KR[None, :] < rs[:, None] + 8)
    ws = np.clip(QC - 8, 0, 48)
    col_ok = (KC[None, :] >= ws[:, None]) & (KC[None, :] < ws[:, None] + 16)
    drow = np.clip(KR[None, :] - R[:, None] + 7, 0, 14)
    dcol = np.clip(KC[None, :] - QC[:, None], -15, 15) + 15
    val = rpb[:, drow, dcol]
    return np.where((row_ok & col_ok)[None], val, np.float32(NEG)).astype(np.float32)


def prep_core_tables(inp, L, typ):
    nab = np.empty((L, 128, 13, 8, 128), np.float32)
    for l in range(L):
        rpb = np.asarray(inp["na_rpb"][l], np.float32)
        tl = [na_tile(rpb, typ, 10, 10 + j) for j in (-2, -1, 0, 1, 2)]
        tl += [na_tile(rpb, typ, 0, c) for c in range(4)] + [na_tile(rpb, typ, 1, c) for c in range(4)]
        for t, tile in enumerate(tl):
            nab[l, :, t, :, :] = tile.transpose(1, 0, 2)
    pos = np.arange(TLOC, dtype=np.float32) if typ == 0 else (np.float32(T - 1) - np.arange(TLOC, dtype=np.float32))
    half = 8
    inv = (np.float32(500000.0) ** (-np.arange(half, dtype=np.float32) / np.float32(half))).astype(np.float32)
    ang = (pos[None, :] * inv[:, None]).astype(np.float32)
    cosT = np.ones((128, TLOC), np.float32)
    sinT = np.zeros((128, TLOC), np.float32)
    for blk in range(2):
        for d in range(16):
            cosT[blk * 64 + d] = np.cos(ang[d % 8])
            sinT[blk * 64 + d] = np.sin(ang[d % 8])
    cfa = np.zeros((128, L * CW + CG), np.float32)
    for l in range(L):
        o = l * CW
        bg = np.asarray(inp["b_gate"][l], np.float32)
        for i in range(4):
            cfa[:, o + i * 16:o + i * 16 + 16] = bg[i].reshape(16, 128).T
        for w, (gn, bn) in enumerate((("ln1_g", "ln1_b"), ("ln2_g", "ln2_b"), ("ln3_g", "ln3_b"))):
            cfa[:, o + 64 + w * 32:o + 64 + w * 32 + 16] = np.asarray(inp[gn][l], np.float32).reshape(16, 128).T
            cfa[:, o + 64 + w * 32 + 16:o + 64 + w * 32 + 32] = np.asarray(inp[bn][l], np.float32).reshape(16, 128).T
        cfa[:, o + 160:o + 164] = np.asarray(inp["pool_scale"][l], np.float32).reshape(4, 128).T
        cw = np.asarray(inp["conv_w"][l], np.float32)
        if typ == 1:
            cw = cw[::-1]
        for tap in range(3):
            cfa[:, o + 164 + tap * 4:o + 164 + tap * 4 + 4] = cw[tap].reshape(4, 128).T
    o = L * CW
    p16 = np.arange(16)
    for g4 in range(4):
        w = 2 ** (g4 + 1)
        h2 = w // 2
        if typ == 0:
            cfa[:, o + g4] = 1.0 / w
            cfa[:, o + 8 + g4 * 16:o + 8 + g4 * 16 + 16] = (1.0 / np.minimum(w, p16 + h2))[None, :]
        else:
            cfa[:, o + 4 + g4] = 1.0 / w
            cfa[:, o + 72 + g4 * 16:o + 72 + g4 * 16 + 16] = (1.0 / np.minimum(w, h2 + p16 + 1))[None, :]
    return {"nab_f": nab.reshape(L, 128, 13 * 8 * 128), "cosT": cosT, "sinT": sinT, "cf": cfa}


def prep_shared(inp, L):
    cbf = np.zeros((128, 768), np.float32)
    cbf[:, 0:128] = np.eye(128, dtype=np.float32)
    Rm = np.zeros((128, 128), np.float32)
    for blk in range(2):
        for d in range(8):
            Rm[blk * 64 + d, blk * 64 + d + 8] = -1.0
            Rm[blk * 64 + d + 8, blk * 64 + d] = 1.0
    cbf[:, 128:256] = Rm.T
    q = np.arange(128)[:, None]
    k = np.arange(128)[None, :]
    cbf[:, 256:384] = np.where(k >= q, 0.0, NEG)
    cbf[:, 384:512] = np.where(k <= q, 0.0, NEG)
    cbf[:, 512:640] = 1.0
    cbf[:, 640:768] = 1.0 / 2048.0
    poolw = np.concatenate([np.asarray(inp["pool_w"][l], np.float32).transpose(1, 0, 2).reshape(128, 512) for l in range(L)], axis=1)
    sink = np.asarray(inp["swa_sinks"][:L], np.float32).reshape(1, L * 8)
    return {"cb_f": cbf, "poolw_f": np.ascontiguousarray(poolw), "sinkrow": np.ascontiguousarray(sink)}


def core_tokens(typ):
    return np.arange(TLOC) if typ == 0 else (T - 1 - np.arange(TLOC))


_CACHE = {}


def make_in_maps(inp, L, cores):
    shared = prep_weights(inp, L)
    shared.update(prep_shared(inp, L))
    tabs = [prep_core_tables(inp, L, t) for t in range(2)]
    maps = []
    for c in cores:
        b, typ = c // 2, c % 2
        idx = core_tokens(typ)
        m = dict(shared)
        m.update(tabs[typ])
        m["xin"] = np.ascontiguousarray(np.asarray(inp["x"][b], np.float32)[idx, :].T).reshape(NCH, 128, TLOC)
        m["memT"] = np.ascontiguousarray(np.asarray(inp["mem"][b], np.float32).T).reshape(NCH, 128, 256)
        maps.append(m)
    return maps


def kernel(**inputs):
    if "nc" not in _CACHE:
        _CACHE["nc"] = build_program(DEPTH)[0]
    nc = _CACHE["nc"]
    cores = list(range(8))
    maps = make_in_maps(inputs, DEPTH, cores)
    res = run_bass_kernel_spmd(nc, maps, core_ids=cores)
    out = np.empty((4, T, D), np.float32)
    for c in cores:
        b, typ = c // 2, c % 2
        idx = core_tokens(typ)[:4096]
        out[b, idx, :] = res.results[c]["outT"].reshape(D, 4096).T
    return out
```
